# Optimizing a Trainium2 kernel written in Bass

```python
import jax, jax.numpy as jnp
from jax import lax
import numpy as np

D_MODEL = 1024
BATCH = 8
SEQ = 2048
DEPTH = 2

GRID_W = 64
CTX_LEN = 256
D_LRU = D_MODEL // 2
D_CONV = D_MODEL - D_LRU
LRU_HEADS = 8
LRU_HEAD_DIM = D_LRU // LRU_HEADS
CONV_GROUPS = 8
D_IN = 2 * D_LRU + 3 * D_CONV
D_FF = 4 * D_MODEL
LRU_CONV_W = 4
SHORT_CONV_W = 3
RG_C = 8.0
N_MOD = 6
EPS = 1e-6

kernel_name = 'hybrid_rglru_shortconv_prefix_dit_block'


def _rmsnorm(x, g):
    xf = x.astype(jnp.float32)
    y = xf * lax.rsqrt(jnp.mean(xf * xf, axis=-1, keepdims=True) + EPS)
    return (y * g.astype(jnp.float32)).astype(x.dtype)


def _modulate(h, shift, scale):
    return h * (1 + scale) + shift


def _dwconv1d(v, w, pad_lo, pad_hi):
    return lax.conv_general_dilated(v, w[:, None, :].astype(v.dtype), window_strides=(1,),
                                    padding=[(pad_lo, pad_hi)],
                                    dimension_numbers=('NWC', 'WIO', 'NWC'),
                                    feature_group_count=v.shape[-1])


def _dwconv2d(v, w):
    kh, kw, ch = w.shape
    return lax.conv_general_dilated(v, w[:, :, None, :].astype(v.dtype), window_strides=(1, 1),
                                    padding=[((kh - 1) // 2, kh // 2), ((kw - 1) // 2, kw // 2)],
                                    dimension_numbers=('NHWC', 'HWIO', 'NHWC'),
                                    feature_group_count=ch)


def _combine(e1, e2):
    a1, b1 = e1
    a2, b2 = e2
    return a1 * a2, a2 * b1 + b2


def _linear_scan(a, b, h0, reverse):
    if h0 is not None:
        edge = -1 if reverse else 0
        b = b.at[:, edge].add(a[:, edge] * h0)
    _, h = lax.associative_scan(_combine, (a, b), reverse=reverse, axis=1)
    return h


def _rglru_coeffs(v, w_a, b_a, w_x, b_x, lam):
    vf = v.astype(jnp.float32)
    vh = vf.reshape(vf.shape[:-1] + (LRU_HEADS, LRU_HEAD_DIM))
    r = jax.nn.sigmoid(jnp.einsum('bshd,hde->bshe', vh, w_a.astype(jnp.float32)).reshape(vf.shape)
                       + b_a.astype(jnp.float32))
    i = jax.nn.sigmoid(jnp.einsum('bshd,hde->bshe', vh, w_x.astype(jnp.float32)).reshape(vf.shape)
                       + b_x.astype(jnp.float32))
    log_a = -RG_C * r * jax.nn.softplus(-lam.astype(jnp.float32))
    a = jnp.exp(log_a)
    b = jnp.sqrt(-jnp.expm1(2.0 * log_a)) * (i * vf)
    return a, b


def _token_mixers(h_lat, h_ctx, w_in, conv4_w, conv4_b, gate_a_w, gate_a_b, gate_x_w, gate_x_b,
                  rg_lambda, conv3_w, g_out_lru, g_out_conv, w_out, ctx_out):
    bsz, seq, _ = h_lat.shape
    rows = seq // GRID_W
    dt = h_lat.dtype
    u_lat = h_lat @ w_in
    u_ctx = h_ctx @ (w_in if ctx_out else w_in[:, :D_LRU])

    v_lat = _dwconv1d(u_lat[..., :D_LRU], conv4_w, 1, 2) + conv4_b
    v_ctx = _dwconv1d(u_ctx[..., :D_LRU], conv4_w, 1, 2) + conv4_b
    hs_lat, hs_ctx = [], []
    for d, rev in enumerate((False, True)):
        a_c, b_c = _rglru_coeffs(v_ctx, gate_a_w[d], gate_a_b[d], gate_x_w[d], gate_x_b[d], rg_lambda[d])
        hc = _linear_scan(a_c, b_c, None, rev)
        h0 = hc[:, 0] if rev else hc[:, -1]
        a_l, b_l = _rglru_coeffs(v_lat, gate_a_w[d], gate_a_b[d], gate_x_w[d], gate_x_b[d], rg_lambda[d])
        hs_lat.append(_linear_scan(a_l, b_l, h0, rev))
        if ctx_out:
            hs_ctx.append(hc)
    y_lru_lat = jax.nn.gelu(u_lat[..., D_LRU:2 * D_LRU]) * (hs_lat[0] + hs_lat[1]).astype(dt)

    o = 2 * D_LRU
    xc, bg, cg = u_lat[..., o:o + D_CONV], u_lat[..., o + D_CONV:o + 2 * D_CONV], u_lat[..., o + 2 * D_CONV:]
    half = D_CONV // 2
    v = (cg * xc).reshape(bsz, rows, GRID_W, D_CONV)
    conv_row = _dwconv2d(v[..., :half], conv3_w[None, :, :half])
    conv_col = _dwconv2d(v[..., half:], conv3_w[:, None, half:])
    y_conv_lat = bg * jnp.concatenate([conv_row, conv_col], axis=-1).reshape(bsz, seq, D_CONV)

    out_lat = jnp.concatenate([_rmsnorm(y_lru_lat, g_out_lru), _rmsnorm(y_conv_lat, g_out_conv)], axis=-1) @ w_out
    if not ctx_out:
        return out_lat, None

    y_lru_ctx = jax.nn.gelu(u_ctx[..., D_LRU:2 * D_LRU]) * (hs_ctx[0] + hs_ctx[1]).astype(dt)
    xc_c, bg_c, cg_c = u_ctx[..., o:o + D_CONV], u_ctx[..., o + D_CONV:o + 2 * D_CONV], u_ctx[..., o + 2 * D_CONV:]
    y_conv_ctx = bg_c * _dwconv1d(cg_c * xc_c, conv3_w, 1, 1)
    out_ctx = jnp.concatenate([_rmsnorm(y_lru_ctx, g_out_lru), _rmsnorm(y_conv_ctx, g_out_conv)], axis=-1) @ w_out
    return out_lat, out_ctx


def _sq_relu_mlp(h, w1, w2):
    return jnp.square(jax.nn.relu(h @ w1)) @ w2


def setup_inputs(seed: int = 0) -> dict:
    key = jax.random.key(seed)
    ks = jax.random.split(key, 24)
    f32 = jnp.float32

    def nrm(k, shape, scale):
        return jax.random.normal(k, shape, f32) * scale

    u = jax.random.uniform(ks[13], (DEPTH, 2, D_LRU), f32, 0.9, 0.999)
    a_base = u ** (1.0 / RG_C)
    return {
        'x': nrm(ks[0], (BATCH, SEQ, D_MODEL), 1.0),
        'c': nrm(ks[1], (BATCH, D_MODEL), 1.0),
        'ctx': nrm(ks[2], (BATCH, CTX_LEN, D_MODEL), 1.0),
        'c_ctx': nrm(ks[3], (D_MODEL,), 1.0),
        'ada_w': nrm(ks[4], (DEPTH, D_MODEL, N_MOD * D_MODEL), 0.5 * D_MODEL ** -0.5),
        'ada_b': nrm(ks[5], (DEPTH, N_MOD * D_MODEL), 0.02),
        'norm1_g': 1.0 + nrm(ks[6], (DEPTH, D_MODEL), 0.02),
        'norm2_g': 1.0 + nrm(ks[7], (DEPTH, D_MODEL), 0.02),
        'w_in': nrm(ks[8], (DEPTH, D_MODEL, D_IN), D_MODEL ** -0.5),
        'conv4_w': nrm(ks[9], (DEPTH, LRU_CONV_W, D_LRU), LRU_CONV_W ** -0.5),
        'conv4_b': nrm(ks[10], (DEPTH, D_LRU), 0.02),
        'gate_a_w': nrm(ks[11], (DEPTH, 2, LRU_HEADS, LRU_HEAD_DIM, LRU_HEAD_DIM), LRU_HEAD_DIM ** -0.5),
        'gate_a_b': nrm(ks[12], (DEPTH, 2, D_LRU), 0.02),
        'gate_x_w': nrm(ks[14], (DEPTH, 2, LRU_HEADS, LRU_HEAD_DIM, LRU_HEAD_DIM), LRU_HEAD_DIM ** -0.5),
        'gate_x_b': nrm(ks[15], (DEPTH, 2, D_LRU), 0.02),
        'rg_lambda': jnp.log(a_base) - jnp.log1p(-a_base),
        'conv3_w': nrm(ks[16], (DEPTH, SHORT_CONV_W, D_CONV), SHORT_CONV_W ** -0.5),
        'g_out_lru': 1.0 + nrm(ks[17], (DEPTH, D_LRU), 0.02),
        'g_out_conv': 1.0 + nrm(ks[18], (DEPTH, D_CONV), 0.02),
        'w_out': nrm(ks[19], (DEPTH, D_MODEL, D_MODEL), D_MODEL ** -0.5),
        'w_mlp1': nrm(ks[20], (DEPTH, D_MODEL, D_FF), D_MODEL ** -0.5),
        'w_mlp2': nrm(ks[21], (DEPTH, D_FF, D_MODEL), D_FF ** -0.5),
        'final_g': 1.0 + nrm(ks[22], (D_MODEL,), 0.02),
    }


def reference(x, c, ctx, c_ctx, ada_w, ada_b, norm1_g, norm2_g, w_in, conv4_w, conv4_b, gate_a_w, gate_a_b,
              gate_x_w, gate_x_b, rg_lambda, conv3_w, g_out_lru, g_out_conv, w_out, w_mlp1, w_mlp2, final_g):
    silu_c = jax.nn.silu(c)
    silu_cc = jax.nn.silu(c_ctx)
    for l in range(DEPTH):
        last = l == DEPTH - 1
        mod_lat = (silu_c @ ada_w[l] + ada_b[l])[:, None, :]
        mod_ctx = silu_cc @ ada_w[l] + ada_b[l]
        sh1, sc1, g1, sh2, sc2, g2 = jnp.split(mod_lat, N_MOD, axis=-1)
        sh1c, sc1c, g1c, sh2c, sc2c, g2c = jnp.split(mod_ctx, N_MOD, axis=-1)

        h_lat = _modulate(_rmsnorm(x, norm1_g[l]), sh1, sc1)
        h_ctx = _modulate(_rmsnorm(ctx, norm1_g[l]), sh1c, sc1c)
        mix_lat, mix_ctx = _token_mixers(h_lat, h_ctx, w_in[l], conv4_w[l], conv4_b[l], gate_a_w[l], gate_a_b[l],
                                         gate_x_w[l], gate_x_b[l], rg_lambda[l], conv3_w[l], g_out_lru[l],
                                         g_out_conv[l], w_out[l], not last)
        x = x + g1 * mix_lat
        x = x + g2 * _sq_relu_mlp(_modulate(_rmsnorm(x, norm2_g[l]), sh2, sc2), w_mlp1[l], w_mlp2[l])
        if not last:
            ctx = ctx + g1c * mix_ctx
            ctx = ctx + g2c * _sq_relu_mlp(_modulate(_rmsnorm(ctx, norm2_g[l]), sh2c, sc2c), w_mlp1[l], w_mlp2[l])
    return _rmsnorm(x, final_g)
```

```python
import numpy as np
import concourse.bass as bass
import concourse.mybir as mybir
from concourse.bass_utils import run_bass_kernel_spmd

F32 = mybir.dt.float32
BF16 = mybir.dt.bfloat16
AF = mybir.ActivationFunctionType
ALU = mybir.AluOpType

D = 1024
S = 2048
CT = 256
T = S + CT
DEPTH = 2
DFF = 4096
DIN = 2560
EPS = 1e-6
TILES = [(0, 512), (512, 512), (1024, 512), (1536, 512), (2048, 256)]
CTX_TI = 4
SB_BASE = 16512
SB_TOP = 229344


def _pcols():
    cols = {}
    off = 0

    def add(name, n):
        nonlocal off
        cols[name] = (off, n)
        off += n

    add("c", 8)
    add("cc", 8)
    add("fg", 8)
    for l in range(DEPTH):
        add(f"adab{l}", 48)
        add(f"n1g{l}", 8)
        add(f"n2g{l}", 8)
        add(f"c4w{l}", 16)
        add(f"c4b{l}", 4)
        add(f"gab{l}", 8)
        add(f"gxb{l}", 8)
        add(f"lam{l}", 8)
        add(f"c3w{l}", 12)
        add(f"gol{l}", 4)
        add(f"goc{l}", 4)
    return cols, off


PCOLS, NP_ = _pcols()


def _chunked(v):
    return np.ascontiguousarray(v.reshape(-1, 128).T)


def _pack_params(b, inp):
    P = np.zeros((128, NP_), np.float32)

    def put(name, arr):
        o, n = PCOLS[name]
        assert arr.shape == (128, n), (name, arr.shape)
        P[:, o:o + n] = arr

    put("c", _chunked(inp["c"][b]))
    put("cc", _chunked(inp["c_ctx"]))
    put("fg", _chunked(inp["final_g"]))
    for l in range(DEPTH):
        put(f"adab{l}", _chunked(inp["ada_b"][l]))
        put(f"n1g{l}", _chunked(inp["norm1_g"][l]))
        put(f"n2g{l}", _chunked(inp["norm2_g"][l]))
        put(f"c4w{l}", inp["conv4_w"][l].reshape(4, 4, 128).transpose(2, 1, 0).reshape(128, 16))
        put(f"c4b{l}", _chunked(inp["conv4_b"][l]))
        for nm, key in (("gab", "gate_a_b"), ("gxb", "gate_x_b"), ("lam", "rg_lambda")):
            put(f"{nm}{l}", inp[key][l].reshape(2, 4, 128).transpose(2, 0, 1).reshape(128, 8))
        put(f"c3w{l}", inp["conv3_w"][l].reshape(3, 4, 128).transpose(2, 1, 0).reshape(128, 12))
        put(f"gol{l}", _chunked(inp["g_out_lru"][l]))
        put(f"goc{l}", _chunked(inp["g_out_conv"][l]))
    return P


class Buf:
    __slots__ = ("name", "last_w", "readers", "dma_sem", "dma_cnt")

    def __init__(self, name):
        self.name = name
        self.last_w = None
        self.readers = []
        self.dma_sem = None
        self.dma_cnt = 0


class _Dummy:
    def __getitem__(self, k):
        return self

    def rearrange(self, *a, **k):
        return self


class Sched:
    ENG = {"pe": "tensor", "act": "scalar", "dve": "vector", "pool": "gpsimd", "sp": "sync"}

    def __init__(self, nc, dry=False):
        self.nc = nc
        self.dry = dry
        self.cnt = {k: 0 for k in self.ENG}
        self.waited = {k: {} for k in self.ENG}
        self.nsem = 0
        self.all_dma = []
        if not dry:
            self.eng = {k: getattr(nc, v) for k, v in self.ENG.items()}
            self.sem = {k: nc.alloc_semaphore(name=f"sem_{k}") for k in self.ENG}

    def _wait(self, e, dep):
        kind, a, v = dep
        if kind == "eng":
            key = ("eng", a)
            sem = self.sem[a]
        else:
            key = ("dma", id(a))
            sem = a
        if self.waited[e].get(key, 0) >= v:
            return
        self.waited[e][key] = v
        self.eng[e].wait_ge(sem, v)

    def _deps(self, e, reads, writes, is_dma=False):
        for b in reads:
            w = b.last_w
            if w is not None:
                self._wait(e, w)
        for b in writes:
            w = b.last_w
            if w is not None and (is_dma or e != "pe" or not (w[0] == "eng" and w[1] == e)):
                self._wait(e, w)
            for r in b.readers:
                if is_dma or e != "pe" or not (r[0] == "eng" and r[1] == e):
                    self._wait(e, r)

    def op(self, e, reads, writes, fn):
        if self.dry:
            return
        self._deps(e, reads, writes)
        inst = fn(self.eng[e])
        self.cnt[e] += 1
        inst.then_inc(self.sem[e], 1)
        tag = ("eng", e, self.cnt[e])
        for b in reads:
            b.readers.append(tag)
        for b in writes:
            b.last_w = tag
            b.readers = []

    def dma(self, q, reads, writes, fn, track):
        if self.dry:
            return None
        self._deps(q, reads, writes, is_dma=True)
        if track.dma_sem is None:
            track.dma_sem = self.nc.alloc_semaphore(name=f"dsem_{self.nsem}")
            self.nsem += 1
            self.all_dma.append(track)
        insts = fn(self.eng[q])
        if not isinstance(insts, (list, tuple)):
            insts = [insts]
        for i in insts:
            i.then_inc(track.dma_sem, 16)
            track.dma_cnt += 16
        tag = ("dma", track.dma_sem, track.dma_cnt)
        for b in reads:
            b.readers.append(tag)
        for b in writes:
            b.last_w = tag
            b.readers = []
        return tag

    def wait_on(self, waiters, targets):
        if self.dry:
            return
        for e in waiters:
            for o in targets:
                if (o != e or e == "pool") and self.cnt[o] > 0:
                    self._wait(e, ("eng", o, self.cnt[o]))

    def barrier(self):
        if self.dry:
            return
        for e in self.ENG:
            for o in self.ENG:
                if o != e and self.cnt[o] > 0:
                    self._wait(e, ("eng", o, self.cnt[o]))
            for tr in self.all_dma:
                if tr.dma_cnt > 0:
                    self._wait(e, ("dma", tr.dma_sem, tr.dma_cnt))


class Ring:
    def __init__(self, sched, slots, plan=None):
        self.S = sched
        self.slots = slots
        self.plan = plan
        self.rec = []
        self.next_load = 0
        self.next_acq = 0
        self.free = list(range(len(slots)))
        self.cap = 8
        self.slot_of = {}

    def pump(self):
        if self.plan is None:
            return
        while self.next_load < len(self.plan):
            lim = self.cap if self.plan.keys[self.next_load][4] else 8
            cand = [s for s in self.free if s < lim]
            if not cand:
                break
            s = cand[0]
            self.free.remove(s)
            k = self.next_load
            self.next_load += 1
            tens, buf = self.slots[s]
            src = self.plan[k]
            self.S.dma("pool", [], [buf], lambda e: e.dma_start(out=tens[:, :, :], in_=src), buf)
            self.slot_of[k] = s

    def acquire(self, src):
        if self.plan is None:
            self.rec.append(src)
            return (len(self.rec) - 1, None, None)
        k = self.next_acq
        self.next_acq += 1
        if k not in self.slot_of:
            self.pump()
        assert k in self.slot_of, "ring deadlock: no free slot"
        tens, buf = self.slots[self.slot_of[k]]
        return (k, tens, buf)

    def release(self, h):
        if self.plan is None:
            return
        s = self.slot_of[h[0]]
        self.free.append(s)
        self.free.sort()
        self.pump()


def _emit(nc, S_, ring_plan, dbg=None):
    dry = S_.dry
    sc = S_

    if not dry:
        xT = nc.dram_tensor("xT", [D, S], F32, kind="ExternalInput").ap()
        cxT = nc.dram_tensor("ctxT", [D, CT], F32, kind="ExternalInput").ap()
        prm_d = nc.dram_tensor("prm", [128, NP_], F32, kind="ExternalInput").ap()
        ada_w = nc.dram_tensor("ada_w", [DEPTH, D, 6 * D], F32, kind="ExternalInput").ap()
        w_in = nc.dram_tensor("w_in", [DEPTH, D, DIN], F32, kind="ExternalInput").ap()
        w_out = nc.dram_tensor("w_out", [DEPTH, D, D], F32, kind="ExternalInput").ap()
        w1 = nc.dram_tensor("w_mlp1", [DEPTH, D, DFF], F32, kind="ExternalInput").ap()
        w2 = nc.dram_tensor("w_mlp2", [DEPTH, DFF, D], F32, kind="ExternalInput").ap()
        gaw = nc.dram_tensor("gate_a_w", [DEPTH, 2, 8, 64, 64], F32, kind="ExternalInput").ap()
        gxw = nc.dram_tensor("gate_x_w", [DEPTH, 2, 8, 64, 64], F32, kind="ExternalInput").ap()
        ident_d = nc.dram_tensor("ident", [128, 128], F32, kind="ExternalInput").ap()
        outT = nc.dram_tensor("outT", [D, S], F32, kind="ExternalOutput").ap()
        if dbg is not None:
            dbgT = nc.dram_tensor("dbgT", [D, T], F32, kind="ExternalOutput").ap()

    tt_state = {"lo": 0, "hi": 0, "mixer": True}

    def slab_src(name, l, r0, c0, hi_ok=None):
        if dry:
            return (name, l, r0, c0, (not tt_state["mixer"]))
        w = {"ada": ada_w, "win": w_in, "wout": w_out, "w1": w1, "w2": w2}[name]
        return w[l][r0:r0 + 1024, c0:c0 + 128].rearrange("(kc p) n -> p kc n", p=128)

    if ring_plan is not None:
        ring_plan.resolver = slab_src

    off = [SB_BASE]

    def alloc(name, shape, dt, at=None):
        nbytes = int(np.prod(shape[1:])) * (4 if dt == F32 else 2)
        if at is None:
            at = off[0]
            off[0] += (nbytes + 31) // 32 * 32
        if dry:
            return _Dummy()
        assert at + nbytes <= SB_TOP, (name, at, nbytes)
        return nc.alloc_sbuf_tensor_at(name, shape, dt, offset=at)

    x_sb = alloc("x_sb", [128, 8, T], F32)
    h_sb = alloc("h_sb", [128, 8, T], BF16)
    y_off = off[0]
    y_sb = alloc("y_sb", [128, 8, T], BF16)
    hid_sb = [alloc(f"hid{i}", [128, 8, 512], BF16, at=y_off + i * 8192) for i in range(2)]
    prm = alloc("prm_sb", [128, NP_], F32)
    mod = [alloc(f"mod{l}", [128, 2, 48], F32) for l in range(DEPTH)]
    gsc1 = [alloc(f"gsc1_{l}", [128, 2, 8], F32) for l in range(DEPTH)]
    gsc2 = [alloc(f"gsc2_{l}", [128, 2, 8], F32) for l in range(DEPTH)]
    hgab = [alloc(f"hgab{l}", [128, 8], F32) for l in range(DEPTH)]
    hgxb = [alloc(f"hgxb{l}", [128, 8], F32) for l in range(DEPTH)]
    cs_sb = [alloc(f"cs{l}", [128, 8], F32) for l in range(DEPTH)]
    hcs_sb = [alloc(f"hcs{l}", [128, 8], F32) for l in range(DEPTH)]
    ptmp = alloc("ptmp", [128, 16], F32)
    scb = alloc("scb", [128, 8, 2], BF16)
    ones = alloc("ones", [128, 128], BF16)
    gw2d = alloc("gw_sb", [128, 2048], BF16)
    gw_sb = gw2d[:, :].rearrange("p (d g j o) -> p d g j o", d=2, g=2, j=4)
    sq_off = off[0]
    sqb = [alloc(f"sqb{i}", [128, 512], BF16) for i in range(2)]
    rl_off = off[0]
    rlb = [alloc(f"rlb{i}", [128, 512], BF16) for i in range(2)]
    ctm_t = [alloc("ctm0", [128, 512], F32, at=sq_off), alloc("ctm1", [128, 512], F32, at=rl_off)]
    epsc = alloc("epsc", [128, 8], F32)
    ident_sb = alloc("ident_sb", [128, 128], BF16)
    ring_slots_t = [alloc(f"slab{i}", [128, 8, 128], BF16) for i in range(8)]
    zone = off[0]
    NHI = 14
    for i in range(NHI):
        ring_slots_t.append(alloc(f"slab{8 + i}", [128, 8, 128], BF16, at=zone + i * 2048))
    tt_hi_off = zone + NHI * 2048
    UBW = 1 + S + 2 + CT + 2
    ub_bytes = (UBW * 2 + 31) // 32 * 32
    ub_sb = alloc("ub_sb", [128, UBW], BF16, at=zone)
    hf_sb = alloc("hf_sb", [128, T], BF16, at=zone + ub_bytes)
    vb_sb = alloc("vb_sb", [128, T], BF16, at=zone + ub_bytes + T * 2)
    tt_lo_off = zone + ub_bytes + T * 4
    NTT_LO = (tt_hi_off - tt_lo_off) // 2048
    tts = []
    for i in range(NTT_LO):
        tts.append(alloc(f"tt{i}", [128, 512], F32, at=tt_lo_off + i * 2048))
    n_lo = len(tts)
    for i in range(4):
        tts.append(alloc(f"tth{i}", [128, 512], F32, at=tt_hi_off + i * 2048))
    vc_sb = alloc("vc_sb", [128, T], BF16, at=tt_hi_off)
    diag_sb = alloc("diag_sb", [128, 4, 128], BF16, at=tt_hi_off + T * 2)
    assert tt_hi_off + T * 2 + 1024 <= tt_hi_off + 3 * 2048
    end = tt_hi_off + 4 * 2048
    assert end <= SB_TOP, end
    assert n_lo >= 4, n_lo

    if not dry:
        ps_t = [nc.alloc_psum_tensor(f"ps{i}", [128, 512], F32) for i in range(8)]
    else:
        ps_t = [_Dummy()] * 8
    ps_b = [Buf(f"ps{i}") for i in range(8)]
    ps_i = [0]

    ps_g = {"a": 0, "b": 0}

    def next_ps(grp=None):
        if grp == "a":
            i = ps_g["a"] % 5
            ps_g["a"] += 1
        elif grp == "b":
            i = 5 + ps_g["b"] % 3
            ps_g["b"] += 1
        elif grp == "u":
            banks = [5, 6, 7]
            i = banks[ps_g.setdefault("u", 0) % len(banks)]
            ps_g["u"] += 1
        elif grp == "c":
            i = 7
        else:
            i = ps_i[0] % 7
            ps_i[0] += 1
        return ps_t[i], ps_b[i]

    xB = [[Buf(f"x{k}_{t}") for t in range(5)] for k in range(8)]
    hB = [[Buf(f"h{k}_{t}") for t in range(5)] for k in range(8)]
    yB = [[Buf(f"y{k}_{t}") for t in range(5)] for k in range(8)]
    hidB = [Buf("hid0"), Buf("hid1")]
    hid_alias = [yB[kc][t] for kc in range(4) for t in range(5)]
    uB = [Buf(f"u{t}") for t in range(5)]
    hfB = [Buf(f"hf{t}") for t in range(5)]
    identB = Buf("ident")

    def ubase(ti):
        t0, _ = TILES[ti]
        return 1 + t0 if ti < 4 else t0 + 3
    vbB = [Buf(f"vb{t}") for t in range(5)]
    ttB = [Buf(f"tt{i}") for i in range(len(tts))]
    sqB = [Buf("sq0"), Buf("sq1")]
    rlB = [Buf("rl0"), Buf("rl1")]
    prmB = Buf("prm")
    modB = [Buf(f"mod{l}") for l in range(DEPTH)]
    derB = Buf("derived")
    onesB = Buf("ones")
    gwB = Buf("gw")
    scbB = Buf("scb")
    ptmpB = Buf("ptmp")
    slotB = [Buf(f"slot{i}") for i in range(8 + NHI)]
    ring = Ring(sc, list(zip(ring_slots_t, slotB)), plan=ring_plan)


    tt_pool = list(range(n_lo)) + [n_lo + 3]

    def next_tt():
        assert tt_state["mixer"]
        i = tt_pool[tt_state["lo"] % len(tt_pool)]
        tt_state["lo"] += 1
        return tts[i], ttB[i]

    def align_tt(m):
        while tt_state["lo"] % m:
            tt_state["lo"] += 1

    def next_rs():
        i = n_lo + tt_state["hi"] % 2
        tt_state["hi"] += 1
        return tts[i], ttB[i]

    ctm_i = [0]

    def next_ctm():
        i = ctm_i[0] % 2
        ctm_i[0] += 1
        return ctm_t[i], ([sqB[0], sqB[1]] if i == 0 else [rlB[0], rlB[1]])

    tm_i = [0]

    def next_tm():
        i = n_lo + 2 + tm_i[0] % 2
        tm_i[0] += 1
        return tts[i], ttB[i]

    sq_i = [0]

    def next_sq():
        i = sq_i[0] % 2
        sq_i[0] += 1
        return sqb[i], sqB[i]

    def pc(name, j=0, n=1):
        o, _ = PCOLS[name]
        return prm[:, o + j:o + j + n]

    sc.dma("sp", [], [prmB], lambda e: e.dma_start(out=prm[:, :], in_=prm_d[:, :]), prmB)
    def load_x_tile(ti, after=()):
        t0, n = TILES[ti]
        src_ = (xT.rearrange("(kc p) t -> p kc t", p=128)[:, :, t0:t0 + n] if ti < 4 else cxT.rearrange("(kc p) t -> p kc t", p=128)) if not dry else None
        sc.dma("sp", list(after), [xB[kc][ti] for kc in range(8)],
               lambda e: e.dma_start(out=x_sb[:, :, t0:t0 + n], in_=src_), xB[0][ti])
    load_x_tile(0)
    sc.op("dve", [], [onesB], lambda e: e.memset(ones[:, :], 1.0))
    sc.dma("pool", [], [identB], lambda e: e.dma_start(out=ident_sb[:, :], in_=ident_d[:, :]), identB)

    def silu_ops():
        sc.op("act", [prmB], [ptmpB], lambda e: e.activation(out=ptmp[:, 0:16], in_=pc("c", 0, 16), func=AF.Tanh, scale=0.5))
        sc.op("dve", [ptmpB], [ptmpB], lambda e: e.tensor_scalar(out=ptmp[:, 0:16], in0=ptmp[:, 0:16], scalar1=0.5, scalar2=0.5, op0=ALU.mult, op1=ALU.add))
        for s_ in range(2):
            sc.op("dve", [ptmpB, prmB], [scbB],
                  lambda e: e.tensor_tensor(out=scb[:, :, s_], in0=ptmp[:, 8 * s_:8 * s_ + 8], in1=pc("c", 8 * s_, 8), op=ALU.mult))
    silu_ops()

    for l in range(DEPTH):
        sc.op("dve", [prmB], [derB], lambda e: e.tensor_scalar(out=hgab[l][:, :], in0=pc(f"gab{l}", 0, 8), scalar1=0.5, scalar2=None, op0=ALU.mult))
        sc.op("dve", [prmB], [derB], lambda e: e.tensor_scalar(out=hgxb[l][:, :], in0=pc(f"gxb{l}", 0, 8), scalar1=0.5, scalar2=None, op0=ALU.mult))
        sc.op("act", [prmB, ptmpB], [ptmpB], lambda e: e.activation(out=ptmp[:, 0:8], in_=pc(f"lam{l}", 0, 8), func=AF.Exp, scale=-1.0))
        sc.op("dve", [ptmpB], [ptmpB], lambda e: e.tensor_scalar(out=ptmp[:, 0:8], in0=ptmp[:, 0:8], scalar1=1.0, scalar2=None, op0=ALU.add))
        sc.op("act", [ptmpB], [ptmpB], lambda e: e.activation(out=ptmp[:, 0:8], in_=ptmp[:, 0:8], func=AF.Ln))
        sc.op("dve", [ptmpB], [derB], lambda e: e.tensor_scalar(out=cs_sb[l][:, :], in0=ptmp[:, 0:8], scalar1=-8.0, scalar2=None, op0=ALU.mult))
        sc.op("dve", [ptmpB], [derB], lambda e: e.tensor_scalar(out=hcs_sb[l][:, :], in0=ptmp[:, 0:8], scalar1=-4.0, scalar2=None, op0=ALU.mult))

    def load_gate_w(l):
        sc.op("dve", [], [gwB], lambda e: e.memset(gw2d[:, :], 0.0))
        for d in range(2):
            for g, src in enumerate((gaw, gxw) if not dry else (None, None)):
                sc.dma("pool", [], [gwB], lambda e: [
                    e.dma_start(out=gw_sb[0:64, d, g, :, 0:64], in_=src[l, d, 0::2].rearrange("h d e -> d h e")),
                    e.dma_start(out=gw_sb[64:128, d, g, :, 64:128], in_=src[l, d, 1::2].rearrange("h d e -> d h e")),
                ], gwB)

    modP = [[Buf(f"mod{l}_{p}") for p in range(3)] for l in range(DEPTH)]
    ada_ps = {}

    def ada_part_of(j):
        return 0 if j < 16 else (1 if j < 24 else 2)

    def ada_chunk(l, j, grp=None):
        part = ada_part_of(j)
        if (l, part) not in ada_ps:
            ada_ps[(l, part)] = next_ps(grp) + ((96 * l) if grp == "c" else 0,)
        pst, psb, cb = ada_ps[(l, part)]
        hs = ring.acquire(slab_src("ada", l, 0, j * 128))
        _, st, sb_ = hs

        def mm(e):
            i = None
            for kc in range(8):
                i = e.matmul(pst[:, cb + 2 * j:cb + 2 * j + 2], lhsT=st[:, kc, :], rhs=scb[:, kc, :], start=(kc == 0), stop=(kc == 7))
            return i
        sc.op("pe", [sb_, scbB] if not dry else [], [psb], mm)
        ring.release(hs)
        last_of_part = j in (15, 23, 47)
        if last_of_part:
            j0 = {0: 0, 1: 16, 2: 24}[part]
            j1 = j + 1
            for s_ in range(2):
                sc.op("dve", [psb, prmB], [modP[l][part]],
                      lambda e: e.tensor_tensor(out=mod[l][:, s_, j0:j1], in0=pst[:, cb + 2 * j0 + s_:cb + 2 * j1:2], in1=pc(f"adab{l}", j0, j1 - j0), op=ALU.add))
            for s_ in range(2):
                if part == 0:
                    sc.op("dve", [modP[l][0], prmB], [modP[l][0]],
                          lambda e: e.scalar_tensor_tensor(out=gsc1[l][:, s_, :], in0=mod[l][:, s_, 8:16], scalar=1.0, in1=pc(f"n1g{l}", 0, 8), op0=ALU.add, op1=ALU.mult))
                if part == 2:
                    sc.op("dve", [modP[l][2], prmB], [modP[l][2]],
                          lambda e: e.scalar_tensor_tensor(out=gsc2[l][:, s_, :], in0=mod[l][:, s_, 32:40], scalar=1.0, in1=pc(f"n2g{l}", 0, 8), op0=ALU.add, op1=ALU.mult))

    def rstd_tile(src_sb, srcB, chunks, ti, dim):
        t0, n = TILES[ti]
        pst, psb = next_ps()
        for ci, kc in enumerate(chunks):
            sq, sqb_ = next_sq()
            sc.op("act", [srcB[kc][ti]], [sqb_], lambda e: e.activation(out=sq[:, 0:n], in_=src_sb[:, kc, t0:t0 + n], func=AF.Square))
            sc.op("pe", [sqb_, onesB], [psb], lambda e: e.matmul(pst[:, 0:n], lhsT=ones[:, :], rhs=sq[:, 0:n], start=(ci == 0), stop=(ci == len(chunks) - 1)))
        return rstd_finish(pst, psb, n, dim)

    def stats_tile(src_sb, srcB, chunks, ti):
        t0, n = TILES[ti]
        pst, psb = next_ps()
        for ci, kc in enumerate(chunks):
            sq, sqb_ = next_sq()
            sc.op("act", [srcB[kc][ti]], [sqb_], lambda e: e.activation(out=sq[:, 0:n], in_=src_sb[:, kc, t0:t0 + n], func=AF.Square))
            sc.op("pe", [sqb_, onesB], [psb], lambda e: e.matmul(pst[:, 0:n], lhsT=ones[:, :], rhs=sq[:, 0:n], start=(ci == 0), stop=(ci == len(chunks) - 1)))
        return pst, psb, n

    def rstd_finish(pst, psb, n, dim):
        rs, rsB = next_rs()
        sc.op("act", [psb, epsB], [rsB], lambda e: e.activation(out=rs[:, 0:n], in_=pst[:, 0:n], func=AF.Ln, bias=epsc[:, 0:1], scale=1.0 / dim))
        sc.op("act", [rsB], [rsB], lambda e: e.activation(out=rs[:, 0:n], in_=rs[:, 0:n], func=AF.Exp, scale=-0.5))
        return rs, rsB

    epsB = Buf("eps")
    sc.op("dve", [], [epsB], lambda e: e.memset(epsc[:, 0:1], EPS))
    sc.op("dve", [epsB], [epsB], lambda e: e.memset(epsc[:, 1:2], 0.25))

    def norm_mod(l, which, tiles, between=None):
        gsc = gsc1[l] if which == 1 else gsc2[l]
        shc = 0 if which == 1 else 24
        st_next = stats_tile(x_sb, xB, list(range(8)), tiles[0])
        for i_, ti in enumerate(tiles):
            t0, n = TILES[ti]
            s_ = 0 if ti < 4 else 1
            st_cur = st_next
            if i_ + 1 < len(tiles):
                st_next = stats_tile(x_sb, xB, list(range(8)), tiles[i_ + 1])
            rs, rsB = rstd_finish(st_cur[0], st_cur[1], st_cur[2], D)
            if between is not None:
                between()
            for kc in range(8):
                tm, tmB = next_tm()
                sc.op("dve", [xB[kc][ti], rsB], [tmB], lambda e: e.tensor_tensor(out=tm[:, 0:n], in0=x_sb[:, kc, t0:t0 + n], in1=rs[:, 0:n], op=ALU.mult))
                sc.op("act", [tmB, modP[l][0 if which == 1 else 2]], [hB[kc][ti]],
                      lambda e: e.activation(out=h_sb[:, kc, t0:t0 + n], in_=tm[:, 0:n], func=AF.Identity,
                                             bias=mod[l][:, s_, shc + kc:shc + kc + 1], scale=gsc[:, s_, kc:kc + 1]))

    def mm8(pst, n, st, rhs_sb, t0):
        def f(e):
            i = None
            for kc in range(8):
                i = e.matmul(pst[:, 0:n], lhsT=st[:, kc, :], rhs=rhs_sb[:, kc, t0:t0 + n], start=(kc == 0), stop=(kc == 7))
            return i
        return f

    def dump(src, nchunks=8):
        stB = Buf("dbgst")
        tags = []
        for kc in range(nchunks):
            tags.append(sc.dma("sp", [b for b in xB[kc]] + [b for b in hB[kc]] + [b for b in yB[kc]] + uB + vbB + hfB, [],
                               lambda e: e.dma_start(out=dbgT[kc * 128:(kc + 1) * 128, :], in_=src(kc)), stB))
        return tags

    tags = []
    stB = Buf("store")

    def final_tile(ti):
        t0, n = TILES[ti]
        rs, rsB = rstd_tile(x_sb, xB, list(range(8)), ti, D)
        for kc in range(8):
            sc.op("dve", [xB[kc][ti], rsB, prmB], [xB[kc][ti]], lambda e: e.scalar_tensor_tensor(
                out=x_sb[:, kc, t0:t0 + n], in0=x_sb[:, kc, t0:t0 + n], scalar=pc("fg", kc), in1=rs[:, 0:n], op0=ALU.mult, op1=ALU.mult))
            tags.append(sc.dma("sp" if kc % 2 == 0 else "pool", [xB[kc][ti]], [], lambda e: e.dma_start(out=outT[kc * 128:(kc + 1) * 128, t0:t0 + n], in_=x_sb[:, kc, t0:t0 + n]), stB))

    for j in range(16):
        ada_chunk(0, j, grp="c")
    for ti_ in (1, 2, 3, 4):
        load_x_tile(ti_, after=[modP[0][0]])
    load_gate_w(0)
    ada_todo = []
    vcB = [ttB[n_lo + 0], ttB[n_lo + 0], ttB[n_lo + 1], ttB[n_lo + 1], ttB[n_lo + 2]]

    for l in range(DEPTH):
        last = l == DEPTH - 1
        all_tiles = [0, 1, 2, 3, 4]
        out_tiles = [0, 1, 2, 3] if last else all_tiles
        if l > 0:
            load_gate_w(l)
        if l == 0:
            ada_rest0 = list(range(16, 48))

            def ada_between0():
                for _ in range(7):
                    if ada_rest0:
                        ada_chunk(0, ada_rest0.pop(0), grp="c")
            norm_mod(0, 1, all_tiles, between=ada_between0)
            while ada_rest0:
                ada_chunk(0, ada_rest0.pop(0), grp="c")
        if dbg == f"A{l}":
            break

        sc.op("dve", [], [uB[0]], lambda e: e.memset(ub_sb[:, 0:1], 0.0))
        sc.op("dve", [], [uB[3], uB[4]], lambda e: e.memset(ub_sb[:, 1 + S:3 + S], 0.0))
        sc.op("dve", [], [uB[4]], lambda e: e.memset(ub_sb[:, UBW - 2:UBW], 0.0))
        ps_g["u7"] = (l > 0)
        n_iters = 4 * 15
        ada_per_iter = (len(ada_todo) + n_iters - 1) // n_iters if l == 0 else 0

        def ux_tile(hx_, ti):
            t0, n = TILES[ti]
            pst, psb = next_ps("u")
            sc.op("pe", [hx_[2]] + [hB[kc][ti] for kc in range(8)] if not dry else [], [psb], mm8(pst, n, hx_[1], h_sb, t0))
            sc.op("act", [psb], [uB[ti]], lambda e: e.activation(out=ub_sb[:, ubase(ti):ubase(ti) + n], in_=pst[:, 0:n], func=AF.Copy))

        hx_next = ring.acquire(slab_src("win", l, 0, 0))
        for ti in all_tiles:
            ux_tile(hx_next, ti)
        ring.release(hx_next)

        for j in range(4):
            hg = ring.acquire(slab_src("win", l, 0, 512 + j * 128))
            hxc = ring.acquire(slab_src("win", l, 0, 1024 + j * 128))
            hcg = ring.acquire(slab_src("win", l, 0, 2048 + j * 128))
            hbg = ring.acquire(slab_src("win", l, 0, 1536 + j * 128))
            hx_next = ring.acquire(slab_src("win", l, 0, (j + 1) * 128)) if j < 3 else None

            def conv_stage1(ti, j=j, hxc=hxc, hcg=hcg):
                t0, n = TILES[ti]
                p1, p1B = next_ps("b")
                sc.op("pe", [hxc[2]] + [hB[kc][ti] for kc in range(8)] if not dry else [], [p1B], mm8(p1, n, hxc[1], h_sb, t0))
                p2, p2B = next_ps("b")
                sc.op("pe", [hcg[2]] + [hB[kc][ti] for kc in range(8)] if not dry else [], [p2B], mm8(p2, n, hcg[1], h_sb, t0))
                tm, tmBs = next_ctm()
                sc.op("dve", [p1B], tmBs, lambda e: e.tensor_copy(out=tm[:, 0:n], in_=p1[:, 0:n]))
                sc.op("dve", [p2B] + tmBs, [vcB[ti]], lambda e: e.tensor_tensor(out=vc_sb[:, t0:t0 + n], in0=p2[:, 0:n], in1=tm[:, 0:n], op=ALU.mult))

            def conv_stage2(ti, j=j, hbg=hbg):
                t0, n = TILES[ti]
                tm, tmBs = next_ctm()
                w0, w1_, w2_ = (pc(f"c3w{l}", j * 3 + k) for k in range(3))
                if ti == 4:
                    nb = [vcB[4]]
                    sc.op("dve", nb + [prmB], tmBs, lambda e: e.tensor_scalar(out=tm[:, 0:n], in0=vc_sb[:, t0:t0 + n], scalar1=w1_, scalar2=None, op0=ALU.mult))
                    sc.op("dve", nb + [prmB] + tmBs, tmBs, lambda e: e.scalar_tensor_tensor(
                        out=tm[:, 1:n], in0=vc_sb[:, t0:t0 + n - 1], scalar=w0, in1=tm[:, 1:n], op0=ALU.mult, op1=ALU.add))
                    sc.op("dve", nb + [prmB] + tmBs, tmBs, lambda e: e.scalar_tensor_tensor(
                        out=tm[:, 0:n - 1], in0=vc_sb[:, t0 + 1:t0 + n], scalar=w2_, in1=tm[:, 0:n - 1], op0=ALU.mult, op1=ALU.add))
                elif j < 2:
                    nb = [vcB[ti]]
                    sc.op("dve", nb + [prmB], tmBs, lambda e: e.tensor_scalar(out=tm[:, 0:n], in0=vc_sb[:, t0:t0 + n], scalar1=w1_, scalar2=None, op0=ALU.mult))
                    t3 = tm[:, 0:n].rearrange("p (r w) -> p r w", w=64)
                    u3 = vc_sb[:, t0:t0 + n].rearrange("p (r w) -> p r w", w=64)
                    sc.op("dve", nb + [prmB] + tmBs, tmBs, lambda e: e.scalar_tensor_tensor(
                        out=t3[:, :, 1:64], in0=u3[:, :, 0:63], scalar=w0, in1=t3[:, :, 1:64], op0=ALU.mult, op1=ALU.add))
                    sc.op("dve", nb + [prmB] + tmBs, tmBs, lambda e: e.scalar_tensor_tensor(
                        out=t3[:, :, 0:63], in0=u3[:, :, 1:64], scalar=w2_, in1=t3[:, :, 0:63], op0=ALU.mult, op1=ALU.add))
                else:
                    nb = list({id(vcB[k]): vcB[k] for k in (ti - 1, ti, ti + 1) if 0 <= k < 4}.values())
                    sc.op("dve", nb + [prmB], tmBs, lambda e: e.tensor_scalar(out=tm[:, 0:n], in0=vc_sb[:, t0:t0 + n], scalar1=w1_, scalar2=None, op0=ALU.mult))
                    a = max(t0, 64)
                    b = t0 + n
                    sc.op("dve", nb + [prmB] + tmBs, tmBs, lambda e: e.scalar_tensor_tensor(
                        out=tm[:, a - t0:b - t0], in0=vc_sb[:, a - 64:b - 64], scalar=w0, in1=tm[:, a - t0:b - t0], op0=ALU.mult, op1=ALU.add))
                    a2 = t0
                    b2 = min(t0 + n, S - 64)
                    sc.op("dve", nb + [prmB] + tmBs, tmBs, lambda e: e.scalar_tensor_tensor(
                        out=tm[:, a2 - t0:b2 - t0], in0=vc_sb[:, a2 + 64:b2 + 64], scalar=w2_, in1=tm[:, a2 - t0:b2 - t0], op0=ALU.mult, op1=ALU.add))
                p3, p3B = next_ps("b")
                sc.op("pe", [hbg[2]] + [hB[kc][ti] for kc in range(8)] if not dry else [], [p3B], mm8(p3, n, hbg[1], h_sb, t0))
                sc.op("dve", [p3B] + tmBs, [yB[4 + j][ti]], lambda e: e.tensor_tensor(out=y_sb[:, 4 + j, t0:t0 + n], in0=p3[:, 0:n], in1=tm[:, 0:n], op=ALU.mult))

            items = [(conv_stage1, ti) for ti in out_tiles] + [(conv_stage2, ti) for ti in out_tiles]
            n_s1 = len(out_tiles)
            done = [0]

            def side_work(conv=True):
                if conv and done[0] < len(items):
                    f, ti_ = items[done[0]]
                    f(ti_)
                    done[0] += 1
                    if done[0] == n_s1:
                        ring.release(hxc)
                        ring.release(hcg)
                    if done[0] == len(items):
                        ring.release(hbg)
                for _ in range(ada_per_iter):
                    if ada_todo:
                        al, aj = ada_todo.pop(0)
                        ada_chunk(al, aj, grp="c")

            for ti in all_tiles:
                if ti == 4 and last:
                    continue
                t0, n = TILES[ti]
                psg, psgB = next_ps("u")
                sc.op("pe", [hg[2]] + [hB[kc][ti] for kc in range(8)] if not dry else [], [psgB], mm8(psg, n, hg[1], h_sb, t0))
                sc.op("act", [psgB], [yB[j][ti]], lambda e: e.activation(out=y_sb[:, j, t0:t0 + n], in_=psg[:, 0:n], func=AF.Gelu_apprx_tanh))
            ring.release(hg)
            dgB = ttB[n_lo + 2]
            for k in range(4):
                sc.op("dve", [identB, prmB], [dgB], lambda e: e.tensor_scalar(
                    out=diag_sb[:, k, :], in0=ident_sb[:, :], scalar1=pc(f"c4w{l}", j * 4 + k), scalar2=None, op0=ALU.mult))

            def conv4_tile(ti, j=j, dgB=dgB):
                t0, n = TILES[ti]
                nb = [uB[k] for k in (ti - 1, ti, ti + 1) if 0 <= k < 5]
                pcv, pcvB = next_ps("u")

                def mmc(e, pcv=pcv, ti=ti, n=n):
                    i = None
                    for k in range(4):
                        s0 = ubase(ti) + k - 1
                        i = e.matmul(pcv[:, 0:n], lhsT=diag_sb[:, k, :], rhs=ub_sb[:, s0:s0 + n], start=(k == 0), stop=(k == 3))
                    return i
                sc.op("pe", nb + [dgB], [pcvB], mmc)
                sc.op("dve", [pcvB, prmB], [vbB[ti]], lambda e: e.tensor_scalar(out=vb_sb[:, t0:t0 + n], in0=pcv[:, 0:n], scalar1=pc(f"c4b{l}", j), scalar2=None, op0=ALU.add))

            for ti_ in (4, 0, 1, 2, 3):
                conv4_tile(ti_)
            extras = {}
            if hx_next is not None:
                for ti_ in all_tiles:
                    extras.setdefault(5 + ti_, []).append(lambda ti_=ti_, hx_=hx_next: ux_tile(hx_, ti_))
            it_i = [0]
            pending = []

            def flush_pending():
                while pending:
                    pending.pop(0)()

            def iter_side():
                for f in extras.pop(it_i[0], []):
                    f()
                it_i[0] += 1
                side_work()

            align_tt(4)
            groups_all = [[(0, 4), (0, 0)], [(0, 1), (0, 2)], [(0, 3), (1, 4)], [(1, 3), (1, 2)], [(1, 1), (1, 0)]]
            prev_d = {0: None, 1: None}
            for gidx, grp_t in enumerate(groups_all):
                st = []
                for gi, (d, ti) in enumerate(grp_t):
                    dj = d * 4 + j
                    t0, n = TILES[ti]
                    bi = (gidx % 2) * 2 + gi
                    psa, psaB = ps_t[bi], ps_b[bi]
                    psx, psxB = ps_t[4], ps_b[4]
                    sc.op("pe", [gwB, vbB[ti]], [psaB], lambda e: e.matmul(psa[:, 0:n], lhsT=gw_sb[:, d, 0, j, :], rhs=vb_sb[:, t0:t0 + n], start=True, stop=True))
                    sc.op("pe", [gwB, vbB[ti]], [psxB], lambda e: e.matmul(psx[:, 0:n], lhsT=gw_sb[:, d, 1, j, :], rhs=vb_sb[:, t0:t0 + n], start=True, stop=True))
                    tS, tSB = next_tt()
                    tI, tIB = next_tt()
                    sc.op("act", [psaB, derB], [psaB], lambda e: e.activation(out=psa[:, 0:n], in_=psa[:, 0:n], func=AF.Tanh, bias=hgab[l][:, dj:dj + 1], scale=0.5))
                    sc.op("act", [psaB, derB], [psaB], lambda e: e.activation(out=psa[:, 0:n], in_=psa[:, 0:n], func=AF.Exp, bias=hcs_sb[l][:, dj:dj + 1], scale=hcs_sb[l][:, dj:dj + 1]))
                    sc.op("act", [psaB], [tSB], lambda e: e.activation(out=tS[:, 0:n], in_=psa[:, 0:n], func=AF.Square))
                    sc.op("act", [psxB, derB], [tIB], lambda e: e.activation(out=tI[:, 0:n], in_=psx[:, 0:n], func=AF.Tanh, bias=hgxb[l][:, dj:dj + 1], scale=0.5))
                    st.append((d, ti, t0, n, psa, psaB, tS, tSB, tI, tIB))
                    iter_side()
                for (d, ti, t0, n, psa, psaB, tS, tSB, tI, tIB) in st:
                    sc.op("act", [tSB, epsB], [tSB], lambda e: e.activation(out=tS[:, 0:n], in_=tS[:, 0:n], func=AF.Sqrt, bias=epsc[:, 1:2], scale=-0.25))
                for (d, ti, t0, n, psa, psaB, tS, tSB, tI, tIB) in st:
                    sc.op("dve", [tIB, vbB[ti]], [tIB], lambda e: e.scalar_tensor_tensor(out=tI[:, 0:n], in0=tI[:, 0:n], scalar=1.0, in1=vb_sb[:, t0:t0 + n], op0=ALU.add, op1=ALU.mult))
                    sc.op("pool", [tIB, tSB], [tIB], lambda e: e.tensor_tensor(out=tI[:, 0:n], in0=tI[:, 0:n], in1=tS[:, 0:n], op=ALU.mult))
                for (d, ti, t0, n, psa, psaB, tS, tSB, tI, tIB) in st:
                    prev = prev_d[d]
                    init = 0.0 if prev is None else prev[0]
                    rd = [psaB, tIB] + ([prev[1]] if prev is not None else [])
                    if d == 0:
                        sc.op("dve", rd, [hfB[ti]], lambda e: e.tensor_tensor_scan(
                            out=hf_sb[:, t0:t0 + n], data0=psa[:, 0:n], data1=tI[:, 0:n], initial=init, op0=ALU.mult, op1=ALU.add))
                        prev_d[0] = (hf_sb[:, t0 + n - 1:t0 + n], hfB[ti])
                    else:
                        sc.op("dve", rd, [tSB], lambda e: e.tensor_tensor_scan(
                            out=tS[:, 0:n][:, ::-1], data0=psa[:, 0:n][:, ::-1], data1=tI[:, 0:n][:, ::-1], initial=init, op0=ALU.mult, op1=ALU.add))
                        prev_d[1] = (tS[:, 0:1], tSB)
                        flush_pending()
                        if not (ti == 4 and last):
                            def second_half(tS=tS, tSB=tSB, ti=ti, t0=t0, n=n):
                                sc.op("pool", [hfB[ti], tSB], [tSB], lambda e: e.tensor_tensor(out=tS[:, 0:n], in0=hf_sb[:, t0:t0 + n], in1=tS[:, 0:n], op=ALU.add))
                                sc.op("pool", [yB[j][ti], tSB], [yB[j][ti]], lambda e: e.tensor_tensor(out=y_sb[:, j, t0:t0 + n], in0=y_sb[:, j, t0:t0 + n], in1=tS[:, 0:n], op=ALU.mult))
                            pending.append(second_half)
            flush_pending()
            while extras:
                iter_side()
            if hx_next is not None:
                ring.release(hx_next)
            while done[0] < len(items):
                side_work()
        while l == 0 and ada_todo:
            al, aj = ada_todo.pop(0)
            ada_chunk(al, aj, grp="c")
        if dbg in (f"B{l}", f"C{l}"):
            break

        sc.wait_on(["pool"], ["act", "dve", "pe", "pool"])
        ring.cap = 8 + NHI
        tt_state["mixer"] = False

        hwo = [ring.acquire(slab_src("wout", l, 0, oc * 128)) for oc in range(8)]
        ngroups = [(ti, grp) for ti in out_tiles for grp in range(2)]
        st_next = stats_tile(y_sb, yB, [0, 1, 2, 3], ngroups[0][0])
        for gi_, (ti, grp) in enumerate(ngroups):
            t0, n = TILES[ti]
            chunks = [4 * grp + c for c in range(4)]
            st_cur = st_next
            if gi_ + 1 < len(ngroups):
                ti2, grp2 = ngroups[gi_ + 1]
                st_next = stats_tile(y_sb, yB, [4 * grp2 + c for c in range(4)], ti2)
            rs, rsB = rstd_finish(st_cur[0], st_cur[1], st_cur[2], 512)
            gname = f"gol{l}" if grp == 0 else f"goc{l}"
            for c, kc in enumerate(chunks):
                sc.op("dve", [yB[kc][ti], rsB, prmB], [yB[kc][ti]], lambda e: e.scalar_tensor_tensor(
                    out=y_sb[:, kc, t0:t0 + n], in0=y_sb[:, kc, t0:t0 + n], scalar=pc(gname, c), in1=rs[:, 0:n], op0=ALU.mult, op1=ALU.mult))
        for oc in range(8):
            for ti in out_tiles:
                t0, n = TILES[ti]
                s_ = 0 if ti < 4 else 1
                pst, psb = next_ps()
                sc.op("pe", [hwo[oc][2]] + [yB[kc][ti] for kc in range(8)] if not dry else [], [psb], mm8(pst, n, hwo[oc][1], y_sb, t0))
                sc.op("dve", [psb, xB[oc][ti], modP[l][1]], [xB[oc][ti]], lambda e: e.scalar_tensor_tensor(
                    out=x_sb[:, oc, t0:t0 + n], in0=pst[:, 0:n], scalar=mod[l][:, s_, 16 + oc:17 + oc], in1=x_sb[:, oc, t0:t0 + n], op0=ALU.mult, op1=ALU.add))
                if oc == 7 and dbg != f"D{l}":
                    norm_mod(l, 2, [ti])
            ring.release(hwo[oc])
        if dbg == f"D{l}":
            break

        ring.pump()
        rl_i = 0
        ada_rest_n = list(range(16, 48))

        def ada_next():
            for _ in range(7):
                if ada_rest_n:
                    ada_chunk(l + 1, ada_rest_n.pop(0), grp="c")
        h1_next = [ring.acquire(slab_src("w1", l, 0, hc * 128)) for hc in range(8)]
        hid_sel = {}
        hid_ctr = [0]
        for q in range(4):
            h1 = h1_next
            h2 = [ring.acquire(slab_src("w2", l, q * 1024, oc * 128)) for oc in range(8)]
            def mlp1_tile(ti, h1):
                nonlocal rl_i
                t0, n = TILES[ti]
                hid_sel[ti] = hid_ctr[0] % 2
                hid_ctr[0] += 1
                hid, hdB = hid_sb[hid_sel[ti]], hidB[hid_sel[ti]]
                for hc in range(8):
                    pst, psb = next_ps()
                    sc.op("pe", [h1[hc][2]] + [hB[kc][ti] for kc in range(8)] if not dry else [], [psb], mm8(pst, n, h1[hc][1], h_sb, t0))
                    rl, rlB_ = rlb[rl_i % 2], rlB[rl_i % 2]
                    rl_i += 1
                    sc.op("act", [psb], [rlB_], lambda e: e.activation(out=rl[:, 0:n], in_=pst[:, 0:n], func=AF.Relu))
                    sc.op("dve", [psb, rlB_], [hdB] + hid_alias, lambda e: e.tensor_tensor(out=hid[:, hc, 0:n], in0=pst[:, 0:n], in1=rl[:, 0:n], op=ALU.mult))

            if q == 0:
                mlp1_tile(out_tiles[0], h1)
            for ii, ti in enumerate(out_tiles):
                t0, n = TILES[ti]
                s_ = 0 if ti < 4 else 1
                hid, hdB = hid_sb[hid_sel[ti]], hidB[hid_sel[ti]]
                if ii + 1 < len(out_tiles):
                    mlp1_tile(out_tiles[ii + 1], h1)
                else:
                    for hh in h1:
                        ring.release(hh)
                    if q < 3:
                        h1_next = [ring.acquire(slab_src("w1", l, 0, (q + 1) * 1024 + hc * 128)) for hc in range(8)]
                        mlp1_tile(out_tiles[0], h1_next)
                for oc in range(8):
                    pst, psb = next_ps()

                    def mm2(e, pst=pst, oc=oc, hid=hid, n=n):
                        i = None
                        for hc in range(8):
                            i = e.matmul(pst[:, 0:n], lhsT=h2[oc][1][:, hc, :], rhs=hid[:, hc, 0:n], start=(hc == 0), stop=(hc == 7))
                        return i
                    sc.op("pe", [h2[oc][2], hdB] + hid_alias if not dry else [], [psb], mm2)
                    sc.op("dve", [psb, xB[oc][ti], modP[l][2]], [xB[oc][ti]], lambda e: e.scalar_tensor_tensor(
                        out=x_sb[:, oc, t0:t0 + n], in0=pst[:, 0:n], scalar=mod[l][:, s_, 40 + oc:41 + oc], in1=x_sb[:, oc, t0:t0 + n], op0=ALU.mult, op1=ALU.add))
                if not last and ti == out_tiles[0] and q in (1, 2):
                    for j_ in range(8 * (q - 1), 8 * q):
                        ada_chunk(l + 1, j_, grp="c")
                if q == 3 and dbg is None or (q == 3 and dbg is not None and not last and dbg[1] == "1"):
                    if last:
                        if ti < 4 and dbg is None:
                            final_tile(ti)
                    else:
                        norm_mod(l + 1, 1, [ti], between=ada_next)
            for hh in h2:
                ring.release(hh)
        if not last and (dbg is None or dbg[1] == "1"):
            while ada_rest_n:
                ada_chunk(l + 1, ada_rest_n.pop(0), grp="c")
        if dbg == f"F{l}":
            break
        if not last:
            sc.wait_on(["act", "dve", "pool"], ["pe"])
            ring.cap = 8
            tt_state["mixer"] = True

    if dbg is None:
        pass
    else:
        sc.barrier()
        if dbg[0] in "DF":
            tags = dump(lambda kc: x_sb[:, kc, :])
        elif dbg[0] == "A":
            for kc in range(8):
                sc.op("act", [hB[kc][t] for t in range(5)], [xB[kc][t] for t in range(5)], lambda e: e.activation(out=x_sb[:, kc, :], in_=h_sb[:, kc, :], func=AF.Copy))
            tags = dump(lambda kc: x_sb[:, kc, :])
        else:
            for kc in range(8):
                sc.op("act", [yB[kc][t] for t in range(5)], [xB[kc][t] for t in range(5)], lambda e: e.activation(out=x_sb[:, kc, :], in_=y_sb[:, kc, :], func=AF.Copy))
            tags = dump(lambda kc: x_sb[:, kc, :])
    if not dry and tags:
        sc._wait("sp", tags[-1])
    return ring


class _KeyPlan:
    def __init__(self, keys):
        self.keys = keys
        self.resolver = None

    def __len__(self):
        return len(self.keys)

    def __getitem__(self, k):
        return self.resolver(*self.keys[k])


def build(dbg=None):
    dry = Sched(None, dry=True)
    r = _emit(None, dry, None, dbg=dbg)
    nc = bass.Bass("TRN2", target_bir_lowering=False)
    real = Sched(nc)
    _emit(nc, real, _KeyPlan(r.rec), dbg=dbg)
    return nc


def kernel(**inputs):
    inp = {k: np.asarray(v) for k, v in inputs.items()}
    nc = build()
    shared = {
        "ada_w": np.ascontiguousarray(inp["ada_w"], dtype=np.float32),
        "w_in": np.ascontiguousarray(inp["w_in"], dtype=np.float32),
        "w_out": np.ascontiguousarray(inp["w_out"], dtype=np.float32),
        "w_mlp1": np.ascontiguousarray(inp["w_mlp1"], dtype=np.float32),
        "w_mlp2": np.ascontiguousarray(inp["w_mlp2"], dtype=np.float32),
        "gate_a_w": np.ascontiguousarray(inp["gate_a_w"], dtype=np.float32),
        "gate_x_w": np.ascontiguousarray(inp["gate_x_w"], dtype=np.float32),
        "ident": np.eye(128, dtype=np.float32),
    }
    in_maps = []
    for b in range(8):
        m = dict(shared)
        m["xT"] = np.ascontiguousarray(inp["x"][b].T)
        m["ctxT"] = np.ascontiguousarray(inp["ctx"][b].T)
        m["prm"] = _pack_params(b, inp)
        in_maps.append(m)
    res = run_bass_kernel_spmd(nc, in_maps, core_ids=list(range(8)))
    out = np.stack([np.ascontiguousarray(r["outT"].T) for r in res.results], axis=0)
    return out.astype(np.float32)
```

```python
import numpy as np
import concourse.bass as bass
import concourse.mybir as mybir
from concourse.bass_utils import run_bass_kernel_spmd

F32 = mybir.dt.float32
BF16 = mybir.dt.bfloat16
AF = mybir.ActivationFunctionType
ALU = mybir.AluOpType

D = 1024
S = 2048
CT = 256
T = S + CT
DEPTH = 2
DFF = 4096
DIN = 2560
EPS = 1e-6
TILES = [(0, 512), (512, 512), (1024, 512), (1536, 512), (2048, 256)]
CTX_TI = 4
SB_BASE = 16512
SB_TOP = 229344


def _pcols():
    cols = {}
    off = 0

    def add(name, n):
        nonlocal off
        cols[name] = (off, n)
        off += n

    add("c", 8)
    add("cc", 8)
    add("fg", 8)
    for l in range(DEPTH):
        add(f"adab{l}", 48)
        add(f"n1g{l}", 8)
        add(f"n2g{l}", 8)
        add(f"c4w{l}", 16)
        add(f"c4b{l}", 4)
        add(f"gab{l}", 8)
        add(f"gxb{l}", 8)
        add(f"lam{l}", 8)
        add(f"c3w{l}", 12)
        add(f"gol{l}", 4)
        add(f"goc{l}", 4)
    return cols, off


PCOLS, NP_ = _pcols()


def _chunked(v):
    return np.ascontiguousarray(v.reshape(-1, 128).T)


def _pack_params(b, inp):
    P = np.zeros((128, NP_), np.float32)

    def put(name, arr):
        o, n = PCOLS[name]
        assert arr.shape == (128, n), (name, arr.shape)
        P[:, o:o + n] = arr

    put("c", _chunked(inp["c"][b]))
    put("cc", _chunked(inp["c_ctx"]))
    put("fg", _chunked(inp["final_g"]))
    for l in range(DEPTH):
        put(f"adab{l}", _chunked(inp["ada_b"][l]))
        put(f"n1g{l}", _chunked(inp["norm1_g"][l]))
        put(f"n2g{l}", _chunked(inp["norm2_g"][l]))
        put(f"c4w{l}", inp["conv4_w"][l].reshape(4, 4, 128).transpose(2, 1, 0).reshape(128, 16))
        put(f"c4b{l}", _chunked(inp["conv4_b"][l]))
        for nm, key in (("gab", "gate_a_b"), ("gxb", "gate_x_b"), ("lam", "rg_lambda")):
            put(f"{nm}{l}", inp[key][l].reshape(2, 4, 128).transpose(2, 0, 1).reshape(128, 8))
        put(f"c3w{l}", inp["conv3_w"][l].reshape(3, 4, 128).transpose(2, 1, 0).reshape(128, 12))
        put(f"gol{l}", _chunked(inp["g_out_lru"][l]))
        put(f"goc{l}", _chunked(inp["g_out_conv"][l]))
    return P


class Buf:
    __slots__ = ("name", "last_w", "readers", "dma_sem", "dma_cnt")

    def __init__(self, name):
        self.name = name
        self.last_w = None
        self.readers = []
        self.dma_sem = None
        self.dma_cnt = 0


class _Dummy:
    def __getitem__(self, k):
        return self

    def rearrange(self, *a, **k):
        return self


class Sched:
    ENG = {"pe": "tensor", "act": "scalar", "dve": "vector", "pool": "gpsimd", "sp": "sync"}

    def __init__(self, nc, dry=False):
        self.nc = nc
        self.dry = dry
        self.cnt = {k: 0 for k in self.ENG}
        self.waited = {k: {} for k in self.ENG}
        self.nsem = 0
        self.all_dma = []
        if not dry:
            self.eng = {k: getattr(nc, v) for k, v in self.ENG.items()}
            self.sem = {k: nc.alloc_semaphore(name=f"sem_{k}") for k in self.ENG}

    def _wait(self, e, dep):
        kind, a, v = dep
        if kind == "eng":
            key = ("eng", a)
            sem = self.sem[a]
        else:
            key = ("dma", id(a))
            sem = a
        if self.waited[e].get(key, 0) >= v:
            return
        self.waited[e][key] = v
        self.eng[e].wait_ge(sem, v)

    def _deps(self, e, reads, writes, is_dma=False):
        for b in reads:
            w = b.last_w
            if w is not None:
                self._wait(e, w)
        for b in writes:
            w = b.last_w
            if w is not None and (is_dma or e != "pe" or not (w[0] == "eng" and w[1] == e)):
                self._wait(e, w)
            for r in b.readers:
                if is_dma or e != "pe" or not (r[0] == "eng" and r[1] == e):
                    self._wait(e, r)

    def op(self, e, reads, writes, fn):
        if self.dry:
            return
        self._deps(e, reads, writes)
        inst = fn(self.eng[e])
        self.cnt[e] += 1
        inst.then_inc(self.sem[e], 1)
        tag = ("eng", e, self.cnt[e])
        for b in reads:
            b.readers.append(tag)
        for b in writes:
            b.last_w = tag
            b.readers = []

    def dma(self, q, reads, writes, fn, track):
        if self.dry:
            return None
        self._deps(q, reads, writes, is_dma=True)
        if track.dma_sem is None:
            track.dma_sem = self.nc.alloc_semaphore(name=f"dsem_{self.nsem}")
            self.nsem += 1
            self.all_dma.append(track)
        insts = fn(self.eng[q])
        if not isinstance(insts, (list, tuple)):
            insts = [insts]
        for i in insts:
            i.then_inc(track.dma_sem, 16)
            track.dma_cnt += 16
        tag = ("dma", track.dma_sem, track.dma_cnt)
        for b in reads:
            b.readers.append(tag)
        for b in writes:
            b.last_w = tag
            b.readers = []
        return tag

    def wait_on(self, waiters, targets):
        if self.dry:
            return
        for e in waiters:
            for o in targets:
                if (o != e or e == "pool") and self.cnt[o] > 0:
                    self._wait(e, ("eng", o, self.cnt[o]))

    def barrier(self):
        if self.dry:
            return
        for e in self.ENG:
            for o in self.ENG:
                if o != e and self.cnt[o] > 0:
                    self._wait(e, ("eng", o, self.cnt[o]))
            for tr in self.all_dma:
                if tr.dma_cnt > 0:
                    self._wait(e, ("dma", tr.dma_sem, tr.dma_cnt))


class Ring:
    def __init__(self, sched, slots, plan=None):
        self.S = sched
        self.slots = slots
        self.plan = plan
        self.rec = []
        self.next_load = 0
        self.next_acq = 0
        self.free = list(range(len(slots)))
        self.cap = 8
        self.slot_of = {}

    def pump(self):
        if self.plan is None:
            return
        while self.next_load < len(self.plan):
            lim = self.cap if self.plan.keys[self.next_load][4] else 8
            cand = [s for s in self.free if s < lim]
            if not cand:
                break
            s = cand[0]
            self.free.remove(s)
            k = self.next_load
            self.next_load += 1
            tens, buf = self.slots[s]
            src = self.plan[k]
            self.S.dma("pool", [], [buf], lambda e: e.dma_start(out=tens[:, :, :], in_=src), buf)
            self.slot_of[k] = s

    def acquire(self, src):
        if self.plan is None:
            self.rec.append(src)
            return (len(self.rec) - 1, None, None)
        k = self.next_acq
        self.next_acq += 1
        if k not in self.slot_of:
            self.pump()
        assert k in self.slot_of, "ring deadlock: no free slot"
        tens, buf = self.slots[self.slot_of[k]]
        return (k, tens, buf)

    def release(self, h):
        if self.plan is None:
            return
        s = self.slot_of[h[0]]
        self.free.append(s)
        self.free.sort()
        self.pump()


def _emit(nc, S_, ring_plan, dbg=None):
    dry = S_.dry
    sc = S_

    if not dry:
        xT = nc.dram_tensor("xT", [D, S], F32, kind="ExternalInput").ap()
        cxT = nc.dram_tensor("ctxT", [D, CT], F32, kind="ExternalInput").ap()
        prm_d = nc.dram_tensor("prm", [128, NP_], F32, kind="ExternalInput").ap()
        ada_w = nc.dram_tensor("ada_w", [DEPTH, D, 6 * D], F32, kind="ExternalInput").ap()
        w_in = nc.dram_tensor("w_in", [DEPTH, D, DIN], F32, kind="ExternalInput").ap()
        w_out = nc.dram_tensor("w_out", [DEPTH, D, D], F32, kind="ExternalInput").ap()
        w1 = nc.dram_tensor("w_mlp1", [DEPTH, D, DFF], F32, kind="ExternalInput").ap()
        w2 = nc.dram_tensor("w_mlp2", [DEPTH, DFF, D], F32, kind="ExternalInput").ap()
        gaw = nc.dram_tensor("gate_a_w", [DEPTH, 2, 8, 64, 64], F32, kind="ExternalInput").ap()
        gxw = nc.dram_tensor("gate_x_w", [DEPTH, 2, 8, 64, 64], F32, kind="ExternalInput").ap()
        ident_d = nc.dram_tensor("ident", [128, 128], F32, kind="ExternalInput").ap()
        outT = nc.dram_tensor("outT", [D, S], F32, kind="ExternalOutput").ap()
        if dbg is not None:
            dbgT = nc.dram_tensor("dbgT", [D, T], F32, kind="ExternalOutput").ap()

    tt_state = {"lo": 0, "hi": 0, "mixer": True}

    def slab_src(name, l, r0, c0, hi_ok=None):
        if dry:
            return (name, l, r0, c0, (not tt_state["mixer"]))
        w = {"ada": ada_w, "win": w_in, "wout": w_out, "w1": w1, "w2": w2}[name]
        return w[l][r0:r0 + 1024, c0:c0 + 128].rearrange("(kc p) n -> p kc n", p=128)

    if ring_plan is not None:
        ring_plan.resolver = slab_src

    off = [SB_BASE]

    def alloc(name, shape, dt, at=None):
        nbytes = int(np.prod(shape[1:])) * (4 if dt == F32 else 2)
        if at is None:
            at = off[0]
            off[0] += (nbytes + 31) // 32 * 32
        if dry:
            return _Dummy()
        assert at + nbytes <= SB_TOP, (name, at, nbytes)
        return nc.alloc_sbuf_tensor_at(name, shape, dt, offset=at)

    x_sb = alloc("x_sb", [128, 8, T], F32)
    h_sb = alloc("h_sb", [128, 8, T], BF16)
    y_off = off[0]
    y_sb = alloc("y_sb", [128, 8, T], BF16)
    hid_sb = [alloc(f"hid{i}", [128, 8, 512], BF16, at=y_off + i * 8192) for i in range(2)]
    prm = alloc("prm_sb", [128, NP_], F32)
    mod = [alloc(f"mod{l}", [128, 2, 48], F32) for l in range(DEPTH)]
    gsc1 = [alloc(f"gsc1_{l}", [128, 2, 8], F32) for l in range(DEPTH)]
    gsc2 = [alloc(f"gsc2_{l}", [128, 2, 8], F32) for l in range(DEPTH)]
    hgab = [alloc(f"hgab{l}", [128, 8], F32) for l in range(DEPTH)]
    hgxb = [alloc(f"hgxb{l}", [128, 8], F32) for l in range(DEPTH)]
    cs_sb = [alloc(f"cs{l}", [128, 8], F32) for l in range(DEPTH)]
    hcs_sb = [alloc(f"hcs{l}", [128, 8], F32) for l in range(DEPTH)]
    ptmp = alloc("ptmp", [128, 16], F32)
    scb = alloc("scb", [128, 8, 2], BF16)
    ones = alloc("ones", [128, 128], BF16)
    gw2d = alloc("gw_sb", [128, 2048], BF16)
    gw_sb = gw2d[:, :].rearrange("p (d g j o) -> p d g j o", d=2, g=2, j=4)
    sq_off = off[0]
    sqb = [alloc(f"sqb{i}", [128, 512], BF16) for i in range(2)]
    rl_off = off[0]
    rlb = [alloc(f"rlb{i}", [128, 512], BF16) for i in range(2)]
    ctm_t = [alloc("ctm0", [128, 512], F32, at=sq_off), alloc("ctm1", [128, 512], F32, at=rl_off)]
    epsc = alloc("epsc", [128, 8], F32)
    ident_sb = alloc("ident_sb", [128, 128], BF16)
    ring_slots_t = [alloc(f"slab{i}", [128, 8, 128], BF16) for i in range(8)]
    zone = off[0]
    NHI = 14
    for i in range(NHI):
        ring_slots_t.append(alloc(f"slab{8 + i}", [128, 8, 128], BF16, at=zone + i * 2048))
    tt_hi_off = zone + NHI * 2048
    UBW = 1 + S + 2 + CT + 2
    ub_bytes = (UBW * 2 + 31) // 32 * 32
    ub_sb = alloc("ub_sb", [128, UBW], BF16, at=zone)
    hf_sb = alloc("hf_sb", [128, T], BF16, at=zone + ub_bytes)
    vb_sb = alloc("vb_sb", [128, T], BF16, at=zone + ub_bytes + T * 2)
    tt_lo_off = zone + ub_bytes + T * 4
    NTT_LO = (tt_hi_off - tt_lo_off) // 2048
    tts = []
    for i in range(NTT_LO):
        tts.append(alloc(f"tt{i}", [128, 512], F32, at=tt_lo_off + i * 2048))
    n_lo = len(tts)
    for i in range(4):
        tts.append(alloc(f"tth{i}", [128, 512], F32, at=tt_hi_off + i * 2048))
    vc_sb = alloc("vc_sb", [128, T], BF16, at=tt_hi_off)
    diag_sb = alloc("diag_sb", [128, 4, 128], BF16, at=tt_hi_off + T * 2)
    assert tt_hi_off + T * 2 + 1024 <= tt_hi_off + 3 * 2048
    end = tt_hi_off + 4 * 2048
    assert end <= SB_TOP, end
    assert n_lo >= 4, n_lo

    if not dry:
        ps_t = [nc.alloc_psum_tensor(f"ps{i}", [128, 512], F32) for i in range(8)]
    else:
        ps_t = [_Dummy()] * 8
    ps_b = [Buf(f"ps{i}") for i in range(8)]
    ps_i = [0]

    ps_g = {"a": 0, "b": 0}

    def next_ps(grp=None):
        if grp == "a":
            i = ps_g["a"] % 5
            ps_g["a"] += 1
        elif grp == "b":
            i = 5 + ps_g["b"] % 3
            ps_g["b"] += 1
        elif grp == "u":
            banks = [5, 6, 7]
            i = banks[ps_g.setdefault("u", 0) % len(banks)]
            ps_g["u"] += 1
        elif grp == "c":
            i = 7
        else:
            i = ps_i[0] % 7
            ps_i[0] += 1
        return ps_t[i], ps_b[i]

    xB = [[Buf(f"x{k}_{t}") for t in range(5)] for k in range(8)]
    hB = [[Buf(f"h{k}_{t}") for t in range(5)] for k in range(8)]
    yB = [[Buf(f"y{k}_{t}") for t in range(5)] for k in range(8)]
    hidB = [Buf("hid0"), Buf("hid1")]
    hid_alias = [yB[kc][t] for kc in range(4) for t in range(5)]
    uB = [Buf(f"u{t}") for t in range(5)]
    hfB = [Buf(f"hf{t}") for t in range(5)]
    identB = Buf("ident")

    def ubase(ti):
        t0, _ = TILES[ti]
        return 1 + t0 if ti < 4 else t0 + 3
    vbB = [Buf(f"vb{t}") for t in range(5)]
    ttB = [Buf(f"tt{i}") for i in range(len(tts))]
    sqB = [Buf("sq0"), Buf("sq1")]
    rlB = [Buf("rl0"), Buf("rl1")]
    prmB = Buf("prm")
    modB = [Buf(f"mod{l}") for l in range(DEPTH)]
    derB = Buf("derived")
    onesB = Buf("ones")
    gwB = Buf("gw")
    scbB = Buf("scb")
    ptmpB = Buf("ptmp")
    slotB = [Buf(f"slot{i}") for i in range(8 + NHI)]
    ring = Ring(sc, list(zip(ring_slots_t, slotB)), plan=ring_plan)


    tt_pool = list(range(n_lo)) + [n_lo + 3]

    def next_tt():
        assert tt_state["mixer"]
        i = tt_pool[tt_state["lo"] % len(tt_pool)]
        tt_state["lo"] += 1
        return tts[i], ttB[i]

    def align_tt(m):
        while tt_state["lo"] % m:
            tt_state["lo"] += 1

    def next_rs():
        i = n_lo + tt_state["hi"] % 2
        tt_state["hi"] += 1
        return tts[i], ttB[i]

    ctm_i = [0]

    def next_ctm():
        i = ctm_i[0] % 2
        ctm_i[0] += 1
        return ctm_t[i], ([sqB[0], sqB[1]] if i == 0 else [rlB[0], rlB[1]])

    tm_i = [0]

    def next_tm():
        i = n_lo + 2 + tm_i[0] % 2
        tm_i[0] += 1
        return tts[i], ttB[i]

    sq_i = [0]

    def next_sq():
        i = sq_i[0] % 2
        sq_i[0] += 1
        return sqb[i], sqB[i]

    def pc(name, j=0, n=1):
        o, _ = PCOLS[name]
        return prm[:, o + j:o + j + n]

    sc.dma("sp", [], [prmB], lambda e: e.dma_start(out=prm[:, :], in_=prm_d[:, :]), prmB)
    def load_x_tile(ti, after=()):
        t0, n = TILES[ti]
        src_ = (xT.rearrange("(kc p) t -> p kc t", p=128)[:, :, t0:t0 + n] if ti < 4 else cxT.rearrange("(kc p) t -> p kc t", p=128)) if not dry else None
        sc.dma("sp", list(after), [xB[kc][ti] for kc in range(8)],
               lambda e: e.dma_start(out=x_sb[:, :, t0:t0 + n], in_=src_), xB[0][ti])
    load_x_tile(0)
    sc.op("dve", [], [onesB], lambda e: e.memset(ones[:, :], 1.0))
    sc.dma("pool", [], [identB], lambda e: e.dma_start(out=ident_sb[:, :], in_=ident_d[:, :]), identB)

    def silu_ops():
        sc.op("act", [prmB], [ptmpB], lambda e: e.activation(out=ptmp[:, 0:16], in_=pc("c", 0, 16), func=AF.Tanh, scale=0.5))
        sc.op("dve", [ptmpB], [ptmpB], lambda e: e.tensor_scalar(out=ptmp[:, 0:16], in0=ptmp[:, 0:16], scalar1=0.5, scalar2=0.5, op0=ALU.mult, op1=ALU.add))
        for s_ in range(2):
            sc.op("dve", [ptmpB, prmB], [scbB],
                  lambda e: e.tensor_tensor(out=scb[:, :, s_], in0=ptmp[:, 8 * s_:8 * s_ + 8], in1=pc("c", 8 * s_, 8), op=ALU.mult))
    silu_ops()

    for l in range(DEPTH):
        sc.op("dve", [prmB], [derB], lambda e: e.tensor_scalar(out=hgab[l][:, :], in0=pc(f"gab{l}", 0, 8), scalar1=0.5, scalar2=None, op0=ALU.mult))
        sc.op("dve", [prmB], [derB], lambda e: e.tensor_scalar(out=hgxb[l][:, :], in0=pc(f"gxb{l}", 0, 8), scalar1=0.5, scalar2=None, op0=ALU.mult))
        sc.op("act", [prmB, ptmpB], [ptmpB], lambda e: e.activation(out=ptmp[:, 0:8], in_=pc(f"lam{l}", 0, 8), func=AF.Exp, scale=-1.0))
        sc.op("dve", [ptmpB], [ptmpB], lambda e: e.tensor_scalar(out=ptmp[:, 0:8], in0=ptmp[:, 0:8], scalar1=1.0, scalar2=None, op0=ALU.add))
        sc.op("act", [ptmpB], [ptmpB], lambda e: e.activation(out=ptmp[:, 0:8], in_=ptmp[:, 0:8], func=AF.Ln))
        sc.op("dve", [ptmpB], [derB], lambda e: e.tensor_scalar(out=cs_sb[l][:, :], in0=ptmp[:, 0:8], scalar1=-8.0, scalar2=None, op0=ALU.mult))
        sc.op("dve", [ptmpB], [derB], lambda e: e.tensor_scalar(out=hcs_sb[l][:, :], in0=ptmp[:, 0:8], scalar1=-4.0, scalar2=None, op0=ALU.mult))

    def load_gate_w(l):
        sc.op("dve", [], [gwB], lambda e: e.memset(gw2d[:, :], 0.0))
        for d in range(2):
            for g, src in enumerate((gaw, gxw) if not dry else (None, None)):
                sc.dma("pool", [], [gwB], lambda e: [
                    e.dma_start(out=gw_sb[0:64, d, g, :, 0:64], in_=src[l, d, 0::2].rearrange("h d e -> d h e")),
                    e.dma_start(out=gw_sb[64:128, d, g, :, 64:128], in_=src[l, d, 1::2].rearrange("h d e -> d h e")),
                ], gwB)

    modP = [[Buf(f"mod{l}_{p}") for p in range(3)] for l in range(DEPTH)]
    ada_ps = {}

    def ada_part_of(j):
        return 0 if j < 16 else (1 if j < 24 else 2)

    def ada_chunk(l, j, grp=None):
        part = ada_part_of(j)
        if (l, part) not in ada_ps:
            ada_ps[(l, part)] = next_ps(grp) + ((96 * l) if grp == "c" else 0,)
        pst, psb, cb = ada_ps[(l, part)]
        hs = ring.acquire(slab_src("ada", l, 0, j * 128))
        _, st, sb_ = hs

        def mm(e):
            i = None
            for kc in range(8):
                i = e.matmul(pst[:, cb + 2 * j:cb + 2 * j + 2], lhsT=st[:, kc, :], rhs=scb[:, kc, :], start=(kc == 0), stop=(kc == 7))
            return i
        sc.op("pe", [sb_, scbB] if not dry else [], [psb], mm)
        ring.release(hs)
        last_of_part = j in (15, 23, 47)
        if last_of_part:
            j0 = {0: 0, 1: 16, 2: 24}[part]
            j1 = j + 1
            for s_ in range(2):
                sc.op("dve", [psb, prmB], [modP[l][part]],
                      lambda e: e.tensor_tensor(out=mod[l][:, s_, j0:j1], in0=pst[:, cb + 2 * j0 + s_:cb + 2 * j1:2], in1=pc(f"adab{l}", j0, j1 - j0), op=ALU.add))
            for s_ in range(2):
                if part == 0:
                    sc.op("dve", [modP[l][0], prmB], [modP[l][0]],
                          lambda e: e.scalar_tensor_tensor(out=gsc1[l][:, s_, :], in0=mod[l][:, s_, 8:16], scalar=1.0, in1=pc(f"n1g{l}", 0, 8), op0=ALU.add, op1=ALU.mult))
                if part == 2:
                    sc.op("dve", [modP[l][2], prmB], [modP[l][2]],
                          lambda e: e.scalar_tensor_tensor(out=gsc2[l][:, s_, :], in0=mod[l][:, s_, 32:40], scalar=1.0, in1=pc(f"n2g{l}", 0, 8), op0=ALU.add, op1=ALU.mult))

    def rstd_tile(src_sb, srcB, chunks, ti, dim):
        t0, n = TILES[ti]
        pst, psb = next_ps()
        for ci, kc in enumerate(chunks):
            sq, sqb_ = next_sq()
            sc.op("act", [srcB[kc][ti]], [sqb_], lambda e: e.activation(out=sq[:, 0:n], in_=src_sb[:, kc, t0:t0 + n], func=AF.Square))
            sc.op("pe", [sqb_, onesB], [psb], lambda e: e.matmul(pst[:, 0:n], lhsT=ones[:, :], rhs=sq[:, 0:n], start=(ci == 0), stop=(ci == len(chunks) - 1)))
        return rstd_finish(pst, psb, n, dim)

    def stats_tile(src_sb, srcB, chunks, ti):
        t0, n = TILES[ti]
        pst, psb = next_ps()
        for ci, kc in enumerate(chunks):
            sq, sqb_ = next_sq()
            sc.op("act", [srcB[kc][ti]], [sqb_], lambda e: e.activation(out=sq[:, 0:n], in_=src_sb[:, kc, t0:t0 + n], func=AF.Square))
            sc.op("pe", [sqb_, onesB], [psb], lambda e: e.matmul(pst[:, 0:n], lhsT=ones[:, :], rhs=sq[:, 0:n], start=(ci == 0), stop=(ci == len(chunks) - 1)))
        return pst, psb, n

    def rstd_finish(pst, psb, n, dim):
        rs, rsB = next_rs()
        sc.op("act", [psb, epsB], [rsB], lambda e: e.activation(out=rs[:, 0:n], in_=pst[:, 0:n], func=AF.Ln, bias=epsc[:, 0:1], scale=1.0 / dim))
        sc.op("act", [rsB], [rsB], lambda e: e.activation(out=rs[:, 0:n], in_=rs[:, 0:n], func=AF.Exp, scale=-0.5))
        return rs, rsB

    epsB = Buf("eps")
    sc.op("dve", [], [epsB], lambda e: e.memset(epsc[:, 0:1], EPS))
    sc.op("dve", [epsB], [epsB], lambda e: e.memset(epsc[:, 1:2], 0.25))

    def norm_mod(l, which, tiles, between=None):
        gsc = gsc1[l] if which == 1 else gsc2[l]
        shc = 0 if which == 1 else 24
        st_next = stats_tile(x_sb, xB, list(range(8)), tiles[0])
        for i_, ti in enumerate(tiles):
            t0, n = TILES[ti]
            s_ = 0 if ti < 4 else 1
            st_cur = st_next
            if i_ + 1 < len(tiles):
                st_next = stats_tile(x_sb, xB, list(range(8)), tiles[i_ + 1])
            rs, rsB = rstd_finish(st_cur[0], st_cur[1], st_cur[2], D)
            if between is not None:
                between()
            for kc in range(8):
                tm, tmB = next_tm()
                sc.op("dve", [xB[kc][ti], rsB], [tmB], lambda e: e.tensor_tensor(out=tm[:, 0:n], in0=x_sb[:, kc, t0:t0 + n], in1=rs[:, 0:n], op=ALU.mult))
                sc.op("act", [tmB, modP[l][0 if which == 1 else 2]], [hB[kc][ti]],
                      lambda e: e.activation(out=h_sb[:, kc, t0:t0 + n], in_=tm[:, 0:n], func=AF.Identity,
                                             bias=mod[l][:, s_, shc + kc:shc + kc + 1], scale=gsc[:, s_, kc:kc + 1]))

    def mm8(pst, n, st, rhs_sb, t0):
        def f(e):
            i = None
            for kc in range(8):
                i = e.matmul(pst[:, 0:n], lhsT=st[:, kc, :], rhs=rhs_sb[:, kc, t0:t0 + n], start=(kc == 0), stop=(kc == 7))
            return i
        return f

    def dump(src, nchunks=8):
        stB = Buf("dbgst")
        tags = []
        for kc in range(nchunks):
            tags.append(sc.dma("sp", [b for b in xB[kc]] + [b for b in hB[kc]] + [b for b in yB[kc]] + uB + vbB + hfB, [],
                               lambda e: e.dma_start(out=dbgT[kc * 128:(kc + 1) * 128, :], in_=src(kc)), stB))
        return tags

    tags = []
    stB = Buf("store")

    def final_tile(ti):
        t0, n = TILES[ti]
        rs, rsB = rstd_tile(x_sb, xB, list(range(8)), ti, D)
        for kc in range(8):
            sc.op("dve", [xB[kc][ti], rsB, prmB], [xB[kc][ti]], lambda e: e.scalar_tensor_tensor(
                out=x_sb[:, kc, t0:t0 + n], in0=x_sb[:, kc, t0:t0 + n], scalar=pc("fg", kc), in1=rs[:, 0:n], op0=ALU.mult, op1=ALU.mult))
            tags.append(sc.dma("sp" if kc % 2 == 0 else "pool", [xB[kc][ti]], [], lambda e: e.dma_start(out=outT[kc * 128:(kc + 1) * 128, t0:t0 + n], in_=x_sb[:, kc, t0:t0 + n]), stB))

    for j in range(16):
        ada_chunk(0, j, grp="c")
    for ti_ in (1, 2, 3, 4):
        load_x_tile(ti_, after=[modP[0][0]])
    load_gate_w(0)
    ada_todo = []
    vcB = [ttB[n_lo + 0], ttB[n_lo + 0], ttB[n_lo + 1], ttB[n_lo + 1], ttB[n_lo + 2]]

    for l in range(DEPTH):
        last = l == DEPTH - 1
        all_tiles = [0, 1, 2, 3, 4]
        out_tiles = [0, 1, 2, 3] if last else all_tiles
        if l > 0:
            load_gate_w(l)
        if l == 0:
            ada_rest0 = list(range(16, 48))

            def ada_between0():
                for _ in range(7):
                    if ada_rest0:
                        ada_chunk(0, ada_rest0.pop(0), grp="c")
            norm_mod(0, 1, all_tiles, between=ada_between0)
            while ada_rest0:
                ada_chunk(0, ada_rest0.pop(0), grp="c")
        if dbg == f"A{l}":
            break

        sc.op("dve", [], [uB[0]], lambda e: e.memset(ub_sb[:, 0:1], 0.0))
        sc.op("dve", [], [uB[3], uB[4]], lambda e: e.memset(ub_sb[:, 1 + S:3 + S], 0.0))
        sc.op("dve", [], [uB[4]], lambda e: e.memset(ub_sb[:, UBW - 2:UBW], 0.0))
        ps_g["u7"] = (l > 0)
        n_iters = 4 * 15
        ada_per_iter = (len(ada_todo) + n_iters - 1) // n_iters if l == 0 else 0

        def ux_tile(hx_, ti):
            t0, n = TILES[ti]
            pst, psb = next_ps("u")
            sc.op("pe", [hx_[2]] + [hB[kc][ti] for kc in range(8)] if not dry else [], [psb], mm8(pst, n, hx_[1], h_sb, t0))
            sc.op("act", [psb], [uB[ti]], lambda e: e.activation(out=ub_sb[:, ubase(ti):ubase(ti) + n], in_=pst[:, 0:n], func=AF.Copy))

        hx_next = ring.acquire(slab_src("win", l, 0, 0))
        for ti in all_tiles:
            ux_tile(hx_next, ti)
        ring.release(hx_next)

        for j in range(4):
            hg = ring.acquire(slab_src("win", l, 0, 512 + j * 128))
            hxc = ring.acquire(slab_src("win", l, 0, 1024 + j * 128))
            hcg = ring.acquire(slab_src("win", l, 0, 2048 + j * 128))
            hbg = ring.acquire(slab_src("win", l, 0, 1536 + j * 128))
            hx_next = ring.acquire(slab_src("win", l, 0, (j + 1) * 128)) if j < 3 else None

            def conv_stage1(ti, j=j, hxc=hxc, hcg=hcg):
                t0, n = TILES[ti]
                p1, p1B = next_ps("b")
                sc.op("pe", [hxc[2]] + [hB[kc][ti] for kc in range(8)] if not dry else [], [p1B], mm8(p1, n, hxc[1], h_sb, t0))
                p2, p2B = next_ps("b")
                sc.op("pe", [hcg[2]] + [hB[kc][ti] for kc in range(8)] if not dry else [], [p2B], mm8(p2, n, hcg[1], h_sb, t0))
                tm, tmBs = next_ctm()
                sc.op("dve", [p1B], tmBs, lambda e: e.tensor_copy(out=tm[:, 0:n], in_=p1[:, 0:n]))
                sc.op("dve", [p2B] + tmBs, [vcB[ti]], lambda e: e.tensor_tensor(out=vc_sb[:, t0:t0 + n], in0=p2[:, 0:n], in1=tm[:, 0:n], op=ALU.mult))

            def conv_stage2(ti, j=j, hbg=hbg):
                t0, n = TILES[ti]
                tm, tmBs = next_ctm()
                w0, w1_, w2_ = (pc(f"c3w{l}", j * 3 + k) for k in range(3))
                if ti == 4:
                    nb = [vcB[4]]
                    sc.op("dve", nb + [prmB], tmBs, lambda e: e.tensor_scalar(out=tm[:, 0:n], in0=vc_sb[:, t0:t0 + n], scalar1=w1_, scalar2=None, op0=ALU.mult))
                    sc.op("dve", nb + [prmB] + tmBs, tmBs, lambda e: e.scalar_tensor_tensor(
                        out=tm[:, 1:n], in0=vc_sb[:, t0:t0 + n - 1], scalar=w0, in1=tm[:, 1:n], op0=ALU.mult, op1=ALU.add))
                    sc.op("dve", nb + [prmB] + tmBs, tmBs, lambda e: e.scalar_tensor_tensor(
                        out=tm[:, 0:n - 1], in0=vc_sb[:, t0 + 1:t0 + n], scalar=w2_, in1=tm[:, 0:n - 1], op0=ALU.mult, op1=ALU.add))
                elif j < 2:
                    nb = [vcB[ti]]
                    sc.op("dve", nb + [prmB], tmBs, lambda e: e.tensor_scalar(out=tm[:, 0:n], in0=vc_sb[:, t0:t0 + n], scalar1=w1_, scalar2=None, op0=ALU.mult))
                    t3 = tm[:, 0:n].rearrange("p (r w) -> p r w", w=64)
                    u3 = vc_sb[:, t0:t0 + n].rearrange("p (r w) -> p r w", w=64)
                    sc.op("dve", nb + [prmB] + tmBs, tmBs, lambda e: e.scalar_tensor_tensor(
                        out=t3[:, :, 1:64], in0=u3[:, :, 0:63], scalar=w0, in1=t3[:, :, 1:64], op0=ALU.mult, op1=ALU.add))
                    sc.op("dve", nb + [prmB] + tmBs, tmBs, lambda e: e.scalar_tensor_tensor(
                        out=t3[:, :, 0:63], in0=u3[:, :, 1:64], scalar=w2_, in1=t3[:, :, 0:63], op0=ALU.mult, op1=ALU.add))
                else:
                    nb = list({id(vcB[k]): vcB[k] for k in (ti - 1, ti, ti + 1) if 0 <= k < 4}.values())
                    sc.op("dve", nb + [prmB], tmBs, lambda e: e.tensor_scalar(out=tm[:, 0:n], in0=vc_sb[:, t0:t0 + n], scalar1=w1_, scalar2=None, op0=ALU.mult))
                    a = max(t0, 64)
                    b = t0 + n
                    sc.op("dve", nb + [prmB] + tmBs, tmBs, lambda e: e.scalar_tensor_tensor(
                        out=tm[:, a - t0:b - t0], in0=vc_sb[:, a - 64:b - 64], scalar=w0, in1=tm[:, a - t0:b - t0], op0=ALU.mult, op1=ALU.add))
                    a2 = t0
                    b2 = min(t0 + n, S - 64)
                    sc.op("dve", nb + [prmB] + tmBs, tmBs, lambda e: e.scalar_tensor_tensor(
                        out=tm[:, a2 - t0:b2 - t0], in0=vc_sb[:, a2 + 64:b2 + 64], scalar=w2_, in1=tm[:, a2 - t0:b2 - t0], op0=ALU.mult, op1=ALU.add))
                p3, p3B = next_ps("b")
                sc.op("pe", [hbg[2]] + [hB[kc][ti] for kc in range(8)] if not dry else [], [p3B], mm8(p3, n, hbg[1], h_sb, t0))
                sc.op("dve", [p3B] + tmBs, [yB[4 + j][ti]], lambda e: e.tensor_tensor(out=y_sb[:, 4 + j, t0:t0 + n], in0=p3[:, 0:n], in1=tm[:, 0:n], op=ALU.mult))

            items = [(conv_stage1, ti) for ti in out_tiles] + [(conv_stage2, ti) for ti in out_tiles]
            n_s1 = len(out_tiles)
            done = [0]

            def side_work(conv=True):
                if conv and done[0] < len(items):
                    f, ti_ = items[done[0]]
                    f(ti_)
                    done[0] += 1
                    if done[0] == n_s1:
                        ring.release(hxc)
                        ring.release(hcg)
                    if done[0] == len(items):
                        ring.release(hbg)
                for _ in range(ada_per_iter):
                    if ada_todo:
                        al, aj = ada_todo.pop(0)
                        ada_chunk(al, aj, grp="c")

            for ti in all_tiles:
                if ti == 4 and last:
                    continue
                t0, n = TILES[ti]
                psg, psgB = next_ps("u")
                sc.op("pe", [hg[2]] + [hB[kc][ti] for kc in range(8)] if not dry else [], [psgB], mm8(psg, n, hg[1], h_sb, t0))
                sc.op("act", [psgB], [yB[j][ti]], lambda e: e.activation(out=y_sb[:, j, t0:t0 + n], in_=psg[:, 0:n], func=AF.Gelu_apprx_tanh))
            ring.release(hg)
            dgB = ttB[n_lo + 2]
            for k in range(4):
                sc.op("dve", [identB, prmB], [dgB], lambda e: e.tensor_scalar(
                    out=diag_sb[:, k, :], in0=ident_sb[:, :], scalar1=pc(f"c4w{l}", j * 4 + k), scalar2=None, op0=ALU.mult))

            def conv4_tile(ti, j=j, dgB=dgB):
                t0, n = TILES[ti]
                nb = [uB[k] for k in (ti - 1, ti, ti + 1) if 0 <= k < 5]
                pcv, pcvB = next_ps("u")

                def mmc(e, pcv=pcv, ti=ti, n=n):
                    i = None
                    for k in range(4):
                        s0 = ubase(ti) + k - 1
                        i = e.matmul(pcv[:, 0:n], lhsT=diag_sb[:, k, :], rhs=ub_sb[:, s0:s0 + n], start=(k == 0), stop=(k == 3))
                    return i
                sc.op("pe", nb + [dgB], [pcvB], mmc)
                sc.op("dve", [pcvB, prmB], [vbB[ti]], lambda e: e.tensor_scalar(out=vb_sb[:, t0:t0 + n], in0=pcv[:, 0:n], scalar1=pc(f"c4b{l}", j), scalar2=None, op0=ALU.add))

            for ti_ in (4, 0, 1, 2, 3):
                conv4_tile(ti_)
            extras = {}
            if hx_next is not None:
                for ti_ in all_tiles:
                    extras.setdefault(5 + ti_, []).append(lambda ti_=ti_, hx_=hx_next: ux_tile(hx_, ti_))
            it_i = [0]
            pending = []

            def flush_pending():
                while pending:
                    pending.pop(0)()

            def iter_side():
                for f in extras.pop(it_i[0], []):
                    f()
                it_i[0] += 1
                side_work()

            align_tt(4)
            groups_all = [[(0, 4), (0, 0)], [(0, 1), (0, 2)], [(0, 3), (1, 4)], [(1, 3), (1, 2)], [(1, 1), (1, 0)]]
            prev_d = {0: None, 1: None}
            for gidx, grp_t in enumerate(groups_all):
                st = []
                for gi, (d, ti) in enumerate(grp_t):
                    dj = d * 4 + j
                    t0, n = TILES[ti]
                    bi = (gidx % 2) * 2 + gi
                    psa, psaB = ps_t[bi], ps_b[bi]
                    psx, psxB = ps_t[4], ps_b[4]
                    sc.op("pe", [gwB, vbB[ti]], [psaB], lambda e: e.matmul(psa[:, 0:n], lhsT=gw_sb[:, d, 0, j, :], rhs=vb_sb[:, t0:t0 + n], start=True, stop=True))
                    sc.op("pe", [gwB, vbB[ti]], [psxB], lambda e: e.matmul(psx[:, 0:n], lhsT=gw_sb[:, d, 1, j, :], rhs=vb_sb[:, t0:t0 + n], start=True, stop=True))
                    tS, tSB = next_tt()
                    tI, tIB = next_tt()
                    sc.op("act", [psaB, derB], [psaB], lambda e: e.activation(out=psa[:, 0:n], in_=psa[:, 0:n], func=AF.Tanh, bias=hgab[l][:, dj:dj + 1], scale=0.5))
                    sc.op("act", [psaB, derB], [psaB], lambda e: e.activation(out=psa[:, 0:n], in_=psa[:, 0:n], func=AF.Exp, bias=hcs_sb[l][:, dj:dj + 1], scale=hcs_sb[l][:, dj:dj + 1]))
                    sc.op("act", [psaB], [tSB], lambda e: e.activation(out=tS[:, 0:n], in_=psa[:, 0:n], func=AF.Square))
                    sc.op("act", [psxB, derB], [tIB], lambda e: e.activation(out=tI[:, 0:n], in_=psx[:, 0:n], func=AF.Tanh, bias=hgxb[l][:, dj:dj + 1], scale=0.5))
                    st.append((d, ti, t0, n, psa, psaB, tS, tSB, tI, tIB))
                    iter_side()
                for (d, ti, t0, n, psa, psaB, tS, tSB, tI, tIB) in st:
                    sc.op("act", [tSB, epsB], [tSB], lambda e: e.activation(out=tS[:, 0:n], in_=tS[:, 0:n], func=AF.Sqrt, bias=epsc[:, 1:2], scale=-0.25))
                for (d, ti, t0, n, psa, psaB, tS, tSB, tI, tIB) in st:
                    sc.op("dve", [tIB, vbB[ti]], [tIB], lambda e: e.scalar_tensor_tensor(out=tI[:, 0:n], in0=tI[:, 0:n], scalar=1.0, in1=vb_sb[:, t0:t0 + n], op0=ALU.add, op1=ALU.mult))
                    sc.op("pool", [tIB, tSB], [tIB], lambda e: e.tensor_tensor(out=tI[:, 0:n], in0=tI[:, 0:n], in1=tS[:, 0:n], op=ALU.mult))
                for (d, ti, t0, n, psa, psaB, tS, tSB, tI, tIB) in st:
                    prev = prev_d[d]
                    init = 0.0 if prev is None else prev[0]
                    rd = [psaB, tIB] + ([prev[1]] if prev is not None else [])
                    if d == 0:
                        sc.op("dve", rd, [hfB[ti]], lambda e: e.tensor_tensor_scan(
                            out=hf_sb[:, t0:t0 + n], data0=psa[:, 0:n], data1=tI[:, 0:n], initial=init, op0=ALU.mult, op1=ALU.add))
                        prev_d[0] = (hf_sb[:, t0 + n - 1:t0 + n], hfB[ti])
                    else:
                        sc.op("dve", rd, [tSB], lambda e: e.tensor_tensor_scan(
                            out=tS[:, 0:n][:, ::-1], data0=psa[:, 0:n][:, ::-1], data1=tI[:, 0:n][:, ::-1], initial=init, op0=ALU.mult, op1=ALU.add))
                        prev_d[1] = (tS[:, 0:1], tSB)
                        flush_pending()
                        if not (ti == 4 and last):
                            def second_half(tS=tS, tSB=tSB, ti=ti, t0=t0, n=n):
                                sc.op("pool", [hfB[ti], tSB], [tSB], lambda e: e.tensor_tensor(out=tS[:, 0:n], in0=hf_sb[:, t0:t0 + n], in1=tS[:, 0:n], op=ALU.add))
                                sc.op("pool", [yB[j][ti], tSB], [yB[j][ti]], lambda e: e.tensor_tensor(out=y_sb[:, j, t0:t0 + n], in0=y_sb[:, j, t0:t0 + n], in1=tS[:, 0:n], op=ALU.mult))
                            pending.append(second_half)
            flush_pending()
            while extras:
                iter_side()
            if hx_next is not None:
                ring.release(hx_next)
            while done[0] < len(items):
                side_work()
        while l == 0 and ada_todo:
            al, aj = ada_todo.pop(0)
            ada_chunk(al, aj, grp="c")
        if dbg in (f"B{l}", f"C{l}"):
            break

        sc.wait_on(["pool"], ["act", "dve", "pe", "pool"])
        ring.cap = 8 + NHI
        tt_state["mixer"] = False

        hwo = [ring.acquire(slab_src("wout", l, 0, oc * 128)) for oc in range(8)]
        ngroups = [(ti, grp) for ti in out_tiles for grp in range(2)]
        st_next = stats_tile(y_sb, yB, [0, 1, 2, 3], ngroups[0][0])
        for gi_, (ti, grp) in enumerate(ngroups):
            t0, n = TILES[ti]
            chunks = [4 * grp + c for c in range(4)]
            st_cur = st_next
            if gi_ + 1 < len(ngroups):
                ti2, grp2 = ngroups[gi_ + 1]
                st_next = stats_tile(y_sb, yB, [4 * grp2 + c for c in range(4)], ti2)
            rs, rsB = rstd_finish(st_cur[0], st_cur[1], st_cur[2], 512)
            gname = f"gol{l}" if grp == 0 else f"goc{l}"
            for c, kc in enumerate(chunks):
                sc.op("dve", [yB[kc][ti], rsB, prmB], [yB[kc][ti]], lambda e: e.scalar_tensor_tensor(
                    out=y_sb[:, kc, t0:t0 + n], in0=y_sb[:, kc, t0:t0 + n], scalar=pc(gname, c), in1=rs[:, 0:n], op0=ALU.mult, op1=ALU.mult))
        for oc in range(8):
            for ti in out_tiles:
                t0, n = TILES[ti]
                s_ = 0 if ti < 4 else 1
                pst, psb = next_ps()
                sc.op("pe", [hwo[oc][2]] + [yB[kc][ti] for kc in range(8)] if not dry else [], [psb], mm8(pst, n, hwo[oc][1], y_sb, t0))
                sc.op("dve", [psb, xB[oc][ti], modP[l][1]], [xB[oc][ti]], lambda e: e.scalar_tensor_tensor(
                    out=x_sb[:, oc, t0:t0 + n], in0=pst[:, 0:n], scalar=mod[l][:, s_, 16 + oc:17 + oc], in1=x_sb[:, oc, t0:t0 + n], op0=ALU.mult, op1=ALU.add))
                if oc == 7 and dbg != f"D{l}":
                    norm_mod(l, 2, [ti])
            ring.release(hwo[oc])
        if dbg == f"D{l}":
            break

        ring.pump()
        rl_i = 0
        ada_rest_n = list(range(16, 48))

        def ada_next():
            for _ in range(7):
                if ada_rest_n:
                    ada_chunk(l + 1, ada_rest_n.pop(0), grp="c")
        for q in range(4):
            h1 = [ring.acquire(slab_src("w1", l, 0, q * 1024 + hc * 128)) for hc in range(8)]
            h2 = [ring.acquire(slab_src("w2", l, q * 1024, oc * 128)) for oc in range(8)]
            def mlp1_tile(ti, h1=h1):
                nonlocal rl_i
                t0, n = TILES[ti]
                hid, hdB = hid_sb[ti % 2], hidB[ti % 2]
                for hc in range(8):
                    pst, psb = next_ps()
                    sc.op("pe", [h1[hc][2]] + [hB[kc][ti] for kc in range(8)] if not dry else [], [psb], mm8(pst, n, h1[hc][1], h_sb, t0))
                    rl, rlB_ = rlb[rl_i % 2], rlB[rl_i % 2]
                    rl_i += 1
                    sc.op("act", [psb], [rlB_], lambda e: e.activation(out=rl[:, 0:n], in_=pst[:, 0:n], func=AF.Relu))
                    sc.op("dve", [psb, rlB_], [hdB] + hid_alias, lambda e: e.tensor_tensor(out=hid[:, hc, 0:n], in0=pst[:, 0:n], in1=rl[:, 0:n], op=ALU.mult))

            mlp1_tile(out_tiles[0])
            for ii, ti in enumerate(out_tiles):
                t0, n = TILES[ti]
                s_ = 0 if ti < 4 else 1
                hid, hdB = hid_sb[ti % 2], hidB[ti % 2]
                if ii + 1 < len(out_tiles):
                    mlp1_tile(out_tiles[ii + 1])
                for oc in range(8):
                    pst, psb = next_ps()

                    def mm2(e, pst=pst, oc=oc, hid=hid, n=n):
                        i = None
                        for hc in range(8):
                            i = e.matmul(pst[:, 0:n], lhsT=h2[oc][1][:, hc, :], rhs=hid[:, hc, 0:n], start=(hc == 0), stop=(hc == 7))
                        return i
                    sc.op("pe", [h2[oc][2], hdB] + hid_alias if not dry else [], [psb], mm2)
                    sc.op("dve", [psb, xB[oc][ti], modP[l][2]], [xB[oc][ti]], lambda e: e.scalar_tensor_tensor(
                        out=x_sb[:, oc, t0:t0 + n], in0=pst[:, 0:n], scalar=mod[l][:, s_, 40 + oc:41 + oc], in1=x_sb[:, oc, t0:t0 + n], op0=ALU.mult, op1=ALU.add))
                if not last and ti == out_tiles[0] and q in (1, 2):
                    for j_ in range(8 * (q - 1), 8 * q):
                        ada_chunk(l + 1, j_, grp="c")
                if q == 3 and dbg is None or (q == 3 and dbg is not None and not last and dbg[1] == "1"):
                    if last:
                        if ti < 4 and dbg is None:
                            final_tile(ti)
                    else:
                        norm_mod(l + 1, 1, [ti], between=ada_next)
            for hh in h1 + h2:
                ring.release(hh)
        if not last and (dbg is None or dbg[1] == "1"):
            while ada_rest_n:
                ada_chunk(l + 1, ada_rest_n.pop(0), grp="c")
        if dbg == f"F{l}":
            break
        if not last:
            sc.wait_on(["act", "dve", "pool"], ["pe"])
            ring.cap = 8
            tt_state["mixer"] = True

    if dbg is None:
        pass
    else:
        sc.barrier()
        if dbg[0] in "DF":
            tags = dump(lambda kc: x_sb[:, kc, :])
        elif dbg[0] == "A":
            for kc in range(8):
                sc.op("act", [hB[kc][t] for t in range(5)], [xB[kc][t] for t in range(5)], lambda e: e.activation(out=x_sb[:, kc, :], in_=h_sb[:, kc, :], func=AF.Copy))
            tags = dump(lambda kc: x_sb[:, kc, :])
        else:
            for kc in range(8):
                sc.op("act", [yB[kc][t] for t in range(5)], [xB[kc][t] for t in range(5)], lambda e: e.activation(out=x_sb[:, kc, :], in_=y_sb[:, kc, :], func=AF.Copy))
            tags = dump(lambda kc: x_sb[:, kc, :])
    if not dry and tags:
        sc._wait("sp", tags[-1])
    return ring


class _KeyPlan:
    def __init__(self, keys):
        self.keys = keys
        self.resolver = None

    def __len__(self):
        return len(self.keys)

    def __getitem__(self, k):
        return self.resolver(*self.keys[k])


def build(dbg=None):
    dry = Sched(None, dry=True)
    r = _emit(None, dry, None, dbg=dbg)
    nc = bass.Bass("TRN2", target_bir_lowering=False)
    real = Sched(nc)
    _emit(nc, real, _KeyPlan(r.rec), dbg=dbg)
    return nc


def kernel(**inputs):
    inp = {k: np.asarray(v) for k, v in inputs.items()}
    nc = build()
    shared = {
        "ada_w": np.ascontiguousarray(inp["ada_w"], dtype=np.float32),
        "w_in": np.ascontiguousarray(inp["w_in"], dtype=np.float32),
        "w_out": np.ascontiguousarray(inp["w_out"], dtype=np.float32),
        "w_mlp1": np.ascontiguousarray(inp["w_mlp1"], dtype=np.float32),
        "w_mlp2": np.ascontiguousarray(inp["w_mlp2"], dtype=np.float32),
        "gate_a_w": np.ascontiguousarray(inp["gate_a_w"], dtype=np.float32),
        "gate_x_w": np.ascontiguousarray(inp["gate_x_w"], dtype=np.float32),
        "ident": np.eye(128, dtype=np.float32),
    }
    in_maps = []
    for b in range(8):
        m = dict(shared)
        m["xT"] = np.ascontiguousarray(inp["x"][b].T)
        m["ctxT"] = np.ascontiguousarray(inp["ctx"][b].T)
        m["prm"] = _pack_params(b, inp)
        in_maps.append(m)
    res = run_bass_kernel_spmd(nc, in_maps, core_ids=list(range(8)))
    out = np.stack([np.ascontiguousarray(r["outT"].T) for r in res.results], axis=0)
    return out.astype(np.float32)
```
